# Optimizing a Trainium2 kernel written in Bass

```python
import jax, jax.numpy as jnp
from jax import lax
import numpy as np

D_MODEL = 1024
BATCH = 16
SEQ = 256
DEPTH = 4
DEC_BATCH = 2
DEC_SEQ = 2048
PAST_LEN = 256

GRID_W = 64
HEAD_DIM = 64
NA_HEADS = 4
NA_ROWS = 8
NA_COLS = 16
GQ_HEADS = 4
GQ_KV_HEADS = 2
WIN_HEADS = 4
WIN_KV_HEADS = 2
WINDOW = 128
BLOCK = 128
MLA_HEADS = 4
MLA_NOPE = 64
MLA_ROPE = 32
MLA_V = 64
MLA_KV_LORA = 128
MLA_QK = MLA_NOPE + MLA_ROPE
N_BRANCH = 4
BRANCH_W = 256
D_FF = 2816
CONV_W = 3
ROPE_THETA = 10000.0
EPS = 1e-6
NEG_INF = -1e30

IN_SIZES = (NA_HEADS * HEAD_DIM, NA_HEADS * HEAD_DIM, NA_HEADS * HEAD_DIM,
            GQ_HEADS * HEAD_DIM, GQ_KV_HEADS * HEAD_DIM, GQ_KV_HEADS * HEAD_DIM,
            WIN_HEADS * HEAD_DIM, WIN_KV_HEADS * HEAD_DIM, WIN_KV_HEADS * HEAD_DIM,
            MLA_HEADS * MLA_QK, MLA_KV_LORA, MLA_ROPE,
            N_BRANCH * D_MODEL)
IN_SPLITS = tuple(sum(IN_SIZES[:i + 1]) for i in range(len(IN_SIZES) - 1))
IN_COLS = sum(IN_SIZES)

kernel_name = 'hybrid_diffusion_prefix_step'


def _rmsnorm(x, g):
    xf = x.astype(jnp.float32)
    y = xf * lax.rsqrt(jnp.mean(xf * xf, axis=-1, keepdims=True) + EPS)
    return (y * g.astype(jnp.float32)).astype(x.dtype)


def _axial_cos_sin(T, dim):
    half = dim // 2
    inv = ROPE_THETA ** (-jnp.arange(0, half, 2, dtype=jnp.float32) / half)
    t = jnp.arange(T, dtype=jnp.int32)
    row = (t // GRID_W).astype(jnp.float32)[:, None] * inv[None, :]
    col = (t % GRID_W).astype(jnp.float32)[:, None] * inv[None, :]
    ang = jnp.concatenate([row, row, col, col], axis=-1)
    return jnp.cos(ang), jnp.sin(ang)


def _rotate_half(v):
    a, b = jnp.split(v, 2, axis=-1)
    return jnp.concatenate([-b, a], axis=-1)


def _apply_axial_rope(x, cos, sin):
    half = x.shape[-1] // 2
    xf = x.astype(jnp.float32)
    rot = jnp.concatenate([_rotate_half(xf[..., :half]), _rotate_half(xf[..., half:])], axis=-1)
    return (xf * cos[:, None, :] + rot * sin[:, None, :]).astype(x.dtype)


def _rope_tail(x, cos, sin):
    return jnp.concatenate([x[..., :MLA_NOPE], _apply_axial_rope(x[..., MLA_NOPE:], cos, sin)], axis=-1)


def _softmax_with_sink(s, sink):
    m = jnp.maximum(jnp.max(s, axis=-1, keepdims=True), sink)
    e = jnp.exp(s - m)
    return e / (jnp.sum(e, axis=-1, keepdims=True) + jnp.exp(sink - m))


def _dense_attn(q, k, v, sink=None):
    B, Tq, Hq, dk = q.shape
    Hkv = k.shape[2]
    g = Hq // Hkv
    qg = q.reshape(B, Tq, Hkv, g, dk)
    s = jnp.einsum('bqhgd,bkhd->bhgqk', qg, k, preferred_element_type=jnp.float32) * (dk ** -0.5)
    if sink is None:
        p = jax.nn.softmax(s, axis=-1)
    else:
        p = _softmax_with_sink(s, sink.astype(jnp.float32).reshape(1, Hkv, g, 1, 1))
    o = jnp.einsum('bhgqk,bkhd->bqhgd', p.astype(v.dtype), v)
    return o.reshape(B, Tq, Hq * v.shape[-1])


def _blocked_global_attn(q, k, v, k_ctx, v_ctx):
    B, T, Hq, dk = q.shape
    Hkv = k.shape[2]
    g = Hq // Hkv
    dv = v.shape[-1]
    k_all = jnp.concatenate([k_ctx, k], axis=1)
    v_all = jnp.concatenate([v_ctx, v], axis=1)
    qb = q.reshape(B, T // BLOCK, BLOCK, Hkv, g, dk).transpose(1, 0, 2, 3, 4, 5)

    def one_block(qblk):
        s = jnp.einsum('bqhgd,bkhd->bhgqk', qblk, k_all, preferred_element_type=jnp.float32) * (dk ** -0.5)
        p = jax.nn.softmax(s, axis=-1).astype(v_all.dtype)
        return jnp.einsum('bhgqk,bkhd->bqhgd', p, v_all)

    o = lax.map(one_block, qb)
    return o.transpose(1, 0, 2, 3, 4, 5).reshape(B, T, Hq * dv)


def _window_attn(q, k, v, k_ctx, v_ctx, sink):
    B, T, Hq, d = q.shape
    Hkv = k.shape[2]
    g = Hq // Hkv
    nb = T // BLOCK
    pad = ((0, 0), (BLOCK, BLOCK), (0, 0), (0, 0))
    kp = jnp.pad(k, pad)
    vp = jnp.pad(v, pad)

    def band(xp):
        return jnp.concatenate(
            [xp[:, i * BLOCK: i * BLOCK + T].reshape(B, nb, BLOCK, Hkv, xp.shape[-1]) for i in range(3)], axis=2)

    kb = band(kp)
    vb = band(vp)
    qb = q.reshape(B, nb, BLOCK, Hkv, g, d)
    scale = d ** -0.5
    s_loc = jnp.einsum('bnqhgd,bnkhd->bhgnqk', qb, kb, preferred_element_type=jnp.float32) * scale
    qpos = jnp.arange(T).reshape(nb, BLOCK)[:, :, None]
    kpos = (jnp.arange(nb)[:, None] * BLOCK - BLOCK + jnp.arange(3 * BLOCK)[None, :])[:, None, :]
    ok = (jnp.abs(qpos - kpos) <= WINDOW) & (kpos >= 0) & (kpos < T)
    s_loc = jnp.where(ok, s_loc, NEG_INF)
    s_ctx = jnp.einsum('bnqhgd,blhd->bhgnql', qb, k_ctx, preferred_element_type=jnp.float32) * scale
    L = k_ctx.shape[1]
    sk = sink.astype(jnp.float32).reshape(1, Hkv, g, 1, 1, 1)
    p = _softmax_with_sink(jnp.concatenate([s_ctx, s_loc], axis=-1), sk).astype(v.dtype)
    o = (jnp.einsum('bhgnql,blhd->bnqhgd', p[..., :L], v_ctx)
         + jnp.einsum('bhgnqk,bnkhd->bnqhgd', p[..., L:], vb))
    return o.reshape(B, T, Hq * d)


def _neighbourhood_attn(q, k, v, k_ctx, v_ctx, rpb):
    B, T, H, d = q.shape
    rows = T // GRID_W
    wr = min(NA_ROWS, rows)
    n_keys = wr * GRID_W
    r_idx = jnp.arange(rows)
    c_idx = jnp.arange(GRID_W)
    r0 = jnp.clip(r_idx - wr // 2, 0, rows - wr)
    c0 = jnp.clip(c_idx - NA_COLS // 2, 0, GRID_W - NA_COLS)
    key_rows = r0[:, None] + jnp.arange(wr)[None, :]
    kg = k.reshape(B, rows, GRID_W, H, d)[:, key_rows].reshape(B, rows, n_keys, H, d)
    vg = v.reshape(B, rows, GRID_W, H, d)[:, key_rows].reshape(B, rows, n_keys, H, d)
    qg = q.reshape(B, rows, GRID_W, H, d)
    key_r = jnp.repeat(key_rows, GRID_W, axis=1)
    key_c = jnp.broadcast_to(c_idx[None, :], (wr, GRID_W)).reshape(n_keys)
    col_ok = (key_c[None, :] >= c0[:, None]) & (key_c[None, :] < c0[:, None] + NA_COLS)
    dr = key_r - r_idx[:, None] + (NA_ROWS - 1)
    dc = jnp.clip(key_c[None, :] - c_idx[:, None], 1 - NA_COLS, NA_COLS - 1) + (NA_COLS - 1)
    bias = rpb[:, dr[:, None, :], dc[None, :, :]].astype(jnp.float32)
    scale = d ** -0.5
    s_loc = jnp.einsum('brqhd,brkhd->bhrqk', qg, kg, preferred_element_type=jnp.float32) * scale + bias[None]
    s_loc = jnp.where(col_ok[None, None, None], s_loc, NEG_INF)
    s_ctx = jnp.einsum('brqhd,blhd->bhrql', qg, k_ctx, preferred_element_type=jnp.float32) * scale
    L = k_ctx.shape[1]
    p = jax.nn.softmax(jnp.concatenate([s_ctx, s_loc], axis=-1), axis=-1).astype(v.dtype)
    o = (jnp.einsum('bhrql,blhd->brqhd', p[..., :L], v_ctx)
         + jnp.einsum('bhrqk,brkhd->brqhd', p[..., L:], vg))
    return o.reshape(B, T, H * d)


def _adaln(cond, lp):
    m = jax.nn.silu(cond) @ lp['w_ada'] + lp['b_ada']
    return jnp.split(m, 6, axis=-1)


def _mixer_inputs(h, lp):
    B, T = h.shape[0], h.shape[1]
    (qa, ka, va, qb, kb, vb, qc, kc, vc, qd, ckv, kpe, gates) = jnp.split(h @ lp['w_in'], IN_SPLITS, axis=-1)

    def heads(t, n, dd):
        return t.reshape(B, T, n, dd)

    return {
        'qa': _rmsnorm(heads(qa, NA_HEADS, HEAD_DIM), lp['qn_a']),
        'ka': _rmsnorm(heads(ka, NA_HEADS, HEAD_DIM), lp['kn_a']),
        'va': heads(va, NA_HEADS, HEAD_DIM),
        'qb': _rmsnorm(heads(qb, GQ_HEADS, HEAD_DIM), lp['qn_b']),
        'kb': _rmsnorm(heads(kb, GQ_KV_HEADS, HEAD_DIM), lp['kn_b']),
        'vb': heads(vb, GQ_KV_HEADS, HEAD_DIM),
        'qc': _rmsnorm(heads(qc, WIN_HEADS, HEAD_DIM), lp['qn_c']),
        'kc': _rmsnorm(heads(kc, WIN_KV_HEADS, HEAD_DIM), lp['kn_c']),
        'vc': heads(vc, WIN_KV_HEADS, HEAD_DIM),
        'qd': _rmsnorm(heads(qd, MLA_HEADS, MLA_QK), lp['qn_d']),
        'ckv': _rmsnorm(ckv, lp['kvn_d']),
        'kpe': kpe,
        'gates': jax.nn.sigmoid(gates.astype(jnp.float32)).astype(h.dtype).reshape(B, T, N_BRANCH, D_MODEL),
    }


def _mla_kv(ckv, kpe, w_ukv, kn):
    B, T = ckv.shape[0], ckv.shape[1]
    kv = (ckv @ w_ukv).reshape(B, T, MLA_HEADS, MLA_NOPE + MLA_V)
    k_nope, v = kv[..., :MLA_NOPE], kv[..., MLA_NOPE:]
    kpe_h = jnp.broadcast_to(kpe[:, :, None, :], (B, T, MLA_HEADS, MLA_ROPE))
    k = _rmsnorm(jnp.concatenate([k_nope, kpe_h], axis=-1), kn)
    return k, v


def _merge(branches, gates, lp):
    br = jnp.stack(branches, axis=2)
    proj = jnp.einsum('btnc,ncd->btnd', br, lp['w_branch'])
    return jnp.sum(gates * proj, axis=2) @ lp['w_out']


def _conv_ffn(h, lp):
    u = h @ lp['w_up']
    T = u.shape[1]
    half = CONV_W // 2
    up = jnp.pad(u, ((0, 0), (half, half), (0, 0)))
    acc = lp['conv_b'] + up[:, 0:T] * lp['conv_w'][0]
    for j in range(1, CONV_W):
        acc = acc + up[:, j:j + T] * lp['conv_w'][j]
    a, g = jnp.split(acc, 2, axis=-1)
    return (jax.nn.silu(g) * a) @ lp['w_down']


def _context_layer(x, c_ctx, lp):
    sh1, sc1, g1, sh2, sc2, g2 = _adaln(c_ctx[None, None, :], lp)
    h = _rmsnorm(x, lp['norm1']) * (1 + sc1) + sh1
    m = _mixer_inputs(h, lp)
    kd, vd = _mla_kv(m['ckv'], m['kpe'], lp['w_ukv'], lp['kn_d'])
    oa = _dense_attn(m['qa'], m['ka'], m['va'])
    ob = _dense_attn(m['qb'], m['kb'], m['vb'])
    oc = _dense_attn(m['qc'], m['kc'], m['vc'], lp['sink_c'])
    od = _dense_attn(m['qd'], kd, vd)
    x = x + g1 * _merge((oa, ob, oc, od), m['gates'], lp)
    h2 = _rmsnorm(x, lp['norm2']) * (1 + sc2) + sh2
    x = x + g2 * _conv_ffn(h2, lp)
    ctx = (m['ka'], m['va'], m['kb'], m['vb'], m['kc'], m['vc'], m['ckv'], m['kpe'])
    return x, ctx


def _latent_layer(x, c, ctx, lp, cos64, sin64, cos32, sin32):
    cka, cva, ckb, cvb, ckc, cvc, cckv, ckpe = ctx
    sh1, sc1, g1, sh2, sc2, g2 = _adaln(c[:, None, :], lp)
    h = _rmsnorm(x, lp['norm1']) * (1 + sc1) + sh1
    m = _mixer_inputs(h, lp)
    oa = _neighbourhood_attn(m['qa'], m['ka'], m['va'], cka, cva, lp['rpb_a'])
    ob = _blocked_global_attn(_apply_axial_rope(m['qb'], cos64, sin64),
                              _apply_axial_rope(m['kb'], cos64, sin64), m['vb'], ckb, cvb)
    oc = _window_attn(_apply_axial_rope(m['qc'], cos64, sin64),
                      _apply_axial_rope(m['kc'], cos64, sin64), m['vc'], ckc, cvc, lp['sink_c'])
    kd, vd = _mla_kv(m['ckv'], m['kpe'], lp['w_ukv'], lp['kn_d'])
    kd_ctx, vd_ctx = _mla_kv(cckv, ckpe, lp['w_ukv'], lp['kn_d'])
    od = _blocked_global_attn(_rope_tail(m['qd'], cos32, sin32), _rope_tail(kd, cos32, sin32),
                              vd, kd_ctx, vd_ctx)
    x = x + g1 * _merge((oa, ob, oc, od), m['gates'], lp)
    h2 = _rmsnorm(x, lp['norm2']) * (1 + sc2) + sh2
    return x + g2 * _conv_ffn(h2, lp)


def setup_inputs(seed: int = 0) -> dict:
    key = jax.random.key(seed)
    keys = list(jax.random.split(key, 48))
    f32 = jnp.float32

    def nrm(shape, scale=1.0):
        return jax.random.normal(keys.pop(), shape, f32) * scale

    def gain(shape):
        return 1.0 + 0.05 * jax.random.normal(keys.pop(), shape, f32)

    L = PAST_LEN
    return {
        'x_prompt': nrm((BATCH, SEQ, D_MODEL)),
        'x_sample': nrm((DEC_BATCH, DEC_SEQ, D_MODEL)),
        'cache_nat_k': nrm((DEC_BATCH, DEPTH, L, NA_HEADS, HEAD_DIM)),
        'cache_nat_v': nrm((DEC_BATCH, DEPTH, L, NA_HEADS, HEAD_DIM)),
        'cache_gqa_k': nrm((DEC_BATCH, DEPTH, L, GQ_KV_HEADS, HEAD_DIM)),
        'cache_gqa_v': nrm((DEC_BATCH, DEPTH, L, GQ_KV_HEADS, HEAD_DIM)),
        'cache_win_k': nrm((DEC_BATCH, DEPTH, L, WIN_KV_HEADS, HEAD_DIM)),
        'cache_win_v': nrm((DEC_BATCH, DEPTH, L, WIN_KV_HEADS, HEAD_DIM)),
        'cache_mla_ckv': nrm((DEC_BATCH, DEPTH, L, MLA_KV_LORA)),
        'cache_mla_kpe': nrm((DEC_BATCH, DEPTH, L, MLA_ROPE)),
        'c': nrm((DEC_BATCH, D_MODEL)),
        'c_ctx': nrm((D_MODEL,)),
        'w_ada': nrm((DEPTH, D_MODEL, 6 * D_MODEL), 0.5 * D_MODEL ** -0.5),
        'b_ada': nrm((DEPTH, 6 * D_MODEL), 0.1),
        'norm1': gain((DEPTH, D_MODEL)),
        'norm2': gain((DEPTH, D_MODEL)),
        'w_in': nrm((DEPTH, D_MODEL, IN_COLS), D_MODEL ** -0.5),
        'qn_a': gain((DEPTH, HEAD_DIM)),
        'kn_a': gain((DEPTH, HEAD_DIM)),
        'rpb_a': nrm((DEPTH, NA_HEADS, 2 * NA_ROWS - 1, 2 * NA_COLS - 1), 0.1),
        'qn_b': gain((DEPTH, HEAD_DIM)),
        'kn_b': gain((DEPTH, HEAD_DIM)),
        'qn_c': gain((DEPTH, HEAD_DIM)),
        'kn_c': gain((DEPTH, HEAD_DIM)),
        'sink_c': nrm((DEPTH, WIN_HEADS), 0.5),
        'qn_d': gain((DEPTH, MLA_QK)),
        'kn_d': gain((DEPTH, MLA_QK)),
        'kvn_d': gain((DEPTH, MLA_KV_LORA)),
        'w_ukv': nrm((DEPTH, MLA_KV_LORA, MLA_HEADS * (MLA_NOPE + MLA_V)), MLA_KV_LORA ** -0.5),
        'w_branch': nrm((DEPTH, N_BRANCH, BRANCH_W, D_MODEL), BRANCH_W ** -0.5),
        'w_out': nrm((DEPTH, D_MODEL, D_MODEL), D_MODEL ** -0.5),
        'w_up': nrm((DEPTH, D_MODEL, 2 * D_FF), D_MODEL ** -0.5),
        'conv_w': nrm((DEPTH, CONV_W, 2 * D_FF), CONV_W ** -0.5),
        'conv_b': nrm((DEPTH, 2 * D_FF), 0.02),
        'w_down': nrm((DEPTH, D_FF, D_MODEL), D_FF ** -0.5),
    }


def reference(x_prompt, x_sample, cache_nat_k, cache_nat_v, cache_gqa_k, cache_gqa_v,
              cache_win_k, cache_win_v, cache_mla_ckv, cache_mla_kpe, c, c_ctx,
              w_ada, b_ada, norm1, norm2, w_in, qn_a, kn_a, rpb_a, qn_b, kn_b, qn_c, kn_c,
              sink_c, qn_d, kn_d, kvn_d, w_ukv, w_branch, w_out, w_up, conv_w, conv_b, w_down):
    T = x_sample.shape[1]
    cos64, sin64 = _axial_cos_sin(T, HEAD_DIM)
    cos32, sin32 = _axial_cos_sin(T, MLA_ROPE)
    xp = x_prompt
    xs = x_sample
    new = [[] for _ in range(8)]
    for l in range(DEPTH):
        lp = {
            'w_ada': w_ada[l], 'b_ada': b_ada[l], 'norm1': norm1[l], 'norm2': norm2[l],
            'w_in': w_in[l], 'qn_a': qn_a[l], 'kn_a': kn_a[l], 'rpb_a': rpb_a[l],
            'qn_b': qn_b[l], 'kn_b': kn_b[l], 'qn_c': qn_c[l], 'kn_c': kn_c[l],
            'sink_c': sink_c[l], 'qn_d': qn_d[l], 'kn_d': kn_d[l], 'kvn_d': kvn_d[l],
            'w_ukv': w_ukv[l], 'w_branch': w_branch[l], 'w_out': w_out[l],
            'w_up': w_up[l], 'conv_w': conv_w[l], 'conv_b': conv_b[l], 'w_down': w_down[l],
        }
        xp, ctx_new = _context_layer(xp, c_ctx, lp)
        for lst, t in zip(new, ctx_new):
            lst.append(t)
        ctx_cached = (cache_nat_k[:, l], cache_nat_v[:, l], cache_gqa_k[:, l], cache_gqa_v[:, l],
                      cache_win_k[:, l], cache_win_v[:, l], cache_mla_ckv[:, l], cache_mla_kpe[:, l])
        xs = _latent_layer(xs, c, ctx_cached, lp, cos64, sin64, cos32, sin32)
    nat_k = jnp.stack(new[0], axis=1)
    nat_v = jnp.stack(new[1], axis=1)
    gqa_k = jnp.stack(new[2], axis=1)
    gqa_v = jnp.stack(new[3], axis=1)
    win_k = jnp.stack(new[4], axis=1)
    win_v = jnp.stack(new[5], axis=1)
    mla_ckv = jnp.stack(new[6], axis=1)
    mla_kpe = jnp.stack(new[7], axis=1)
    return (xp, xs, nat_k, nat_v, gqa_k, gqa_v, win_k, win_v, mla_ckv, mla_kpe)
```

```python
from contextlib import ExitStack
import numpy as np
import concourse.bass as bass
import concourse.mybir as mybir
from concourse.bass_utils import run_bass_kernel_spmd

F32 = mybir.dt.float32
BF16 = mybir.dt.bfloat16
AF = mybir.ActivationFunctionType
ALU = mybir.AluOpType

ENGS = ("pe", "act", "dve", "pool", "sp")
DMA_RING = 8
DEPTH = 4
NL = DEPTH
EPS = 1e-6


class Buf:
    __slots__ = ("name", "last_w", "readers", "mem")

    def __init__(self, name="", mem=None):
        self.name = name
        self.last_w = None
        self.readers = {}
        self.mem = mem


class Op:
    __slots__ = ("eng", "fn", "deps", "needs_inc", "sem", "semval", "dma", "idx", "inc", "csem")

    def __init__(self, eng, fn, dma, inc=None):
        self.eng = eng
        self.fn = fn
        self.dma = dma
        self.deps = []
        self.needs_inc = False
        self.sem = None
        self.semval = 0
        self.inc = inc
        self.idx = 0
        self.csem = None


class Sched:
    def __init__(self, nc):
        self.nc = nc
        self.ops = {e: [] for e in ENGS}
        self.n_dma = {e: 0 for e in ENGS}
        self._dma_ops = {e: [] for e in ENGS}
        self.live = set()

    def _alias_deps(self, b, deps):
        if b.mem is None or b in self.live:
            return
        lo, hi = b.mem
        for o in list(self.live):
            if o.mem[0] < hi and lo < o.mem[1]:
                if o.last_w is not None:
                    deps[id(o.last_w)] = o.last_w
                for r in o.readers.values():
                    deps[id(r)] = r
                self.live.discard(o)
        self.live.add(b)

    def add(self, eng, fn, reads=(), writes=(), dma=False, extra_deps=(), inc=None, csem=None):
        op = Op(eng, fn, dma, inc)
        op.csem = csem
        deps = {}
        for b in list(reads) + list(writes):
            self._alias_deps(b, deps)
        for b in reads:
            if b.last_w is not None:
                deps[id(b.last_w)] = b.last_w
        for b in writes:
            if b.last_w is not None:
                deps[id(b.last_w)] = b.last_w
            for r in b.readers.values():
                deps[id(r)] = r
        for d in extra_deps:
            deps[id(d)] = d
        for b in reads:
            b.readers[id(op) if dma else eng] = op
        for b in writes:
            b.last_w = op
            b.readers = {}
        op.deps = [d for d in deps.values() if not (d.eng == "pe" and eng == "pe" and not d.dma and not dma)]
        if dma and csem is None:
            op.idx = self.n_dma[eng]
            self.n_dma[eng] += 1
            if op.idx >= DMA_RING:
                op.deps.append(self._dma_ops[eng][op.idx - DMA_RING])
            self._dma_ops[eng].append(op)
        for d in op.deps:
            d.needs_inc = True
        self.ops[eng].append(op)
        return op

    def emit(self):
        nc = self.nc
        with ExitStack() as st:
            esem = {e: st.enter_context(nc.semaphore("s_" + e)) for e in ENGS}
            dsem = {e: [st.enter_context(nc.semaphore("d_%s%d" % (e, i))) for i in range(DMA_RING)]
                    for e in ("sp", "pool")}
            ckeys = sorted({op.csem for e in ENGS for op in self.ops[e] if op.csem is not None})
            csems = {k: st.enter_context(nc.semaphore("c_" + k)) for k in ckeys}
            ccnt = {k: 0 for k in ckeys}
            for e in ENGS:
                cnt = 0
                dcnt = [0] * DMA_RING
                for op in self.ops[e]:
                    if op.csem is not None:
                        ccnt[op.csem] += op.inc
                        op.sem = csems[op.csem]
                        op.semval = ccnt[op.csem]
                        op.needs_inc = True
                    elif op.dma:
                        k = op.idx % DMA_RING
                        dcnt[k] += 16 if op.inc is None else op.inc
                        op.sem = dsem[e][k]
                        op.semval = dcnt[k]
                        op.needs_inc = True
                    elif op.needs_inc:
                        cnt += 1
                        op.sem = esem[e]
                        op.semval = cnt
            block = st.enter_context(nc.Block())

            def make(e):
                def body(eng):
                    waited = {}
                    for op in self.ops[e]:
                        need = {}
                        for d in op.deps:
                            k = id(d.sem)
                            if waited.get(k, 0) >= d.semval:
                                continue
                            if k not in need or need[k][1] < d.semval:
                                need[k] = (d.sem, d.semval)
                        for k, (s, v) in need.items():
                            eng.wait_ge(s, v)
                            waited[k] = v
                        ins = op.fn(eng)
                        if op.needs_inc and ins is not None:
                            ins.then_inc(op.sem, (16 if op.inc is None else op.inc) if op.dma else 1)
                return body

            reg = {"pe": block.tensor, "act": block.scalar, "dve": block.vector,
                   "pool": block.gpsimd, "sp": block.sync}
            for e in ENGS:
                if self.ops[e]:
                    reg[e](make(e))


PV_B = 0
PV_N1 = 48
PV_N2 = 56
PV_CW = 64
PV_CB = 196
PV_QG = 240
PV_KG = 244
PV_SK = 249
PV_L = 251
PV_EPS = NL * PV_L
PV_QS = PV_EPS + 1
PV_ES = PV_QS + NL * 4
PV_N = PV_ES + NL * 2

CM_ONES, CM_BD64, CM_ONES96, CM_SEL, CM_P64, CM_P96 = range(6)
NQKV = 2560


def _host_consts():
    cm = np.zeros((128, 6, 128), np.float32)
    cm[:, CM_ONES, :] = 1.0
    cm[0:64, CM_BD64, 0:64] = 1.0
    cm[64:128, CM_BD64, 64:128] = 1.0
    cm[0:96, CM_ONES96, 0:96] = 1.0
    for i in range(32):
        cm[i, CM_SEL, 64 + i] = 1.0
    def perm(base, sec):
        P = np.zeros((128, 128), np.float32)
        h = sec // 2
        for s0 in base:
            for i in range(h):
                P[s0 + i, s0 + i + h] = -1.0
                P[s0 + i + h, s0 + i] = 1.0
        return P.T.copy()
    cm[:, CM_P64, :] = perm([0, 32, 64, 96], 32)
    cm[:, CM_P96, :] = perm([64, 80], 16)
    return cm.reshape(128, 6 * 128)


NU = 22
KB = 1024


class Prog:
    def __init__(self, tiles=(0, 1)):
        self.tiles = tiles
        self.nc = bass.Bass("TRN2", target_bir_lowering=False)
        self.S = Sched(self.nc)
        self.st = ExitStack()
        self.outs = []
        self.gi = 0
        self.rc = {}
        self.dyn = None
        self.pipe = []

    def A(self, eng, fn, r=(), w=(), **kw):
        return self.S.add(eng, fn, reads=r, writes=w, **kw)

    def din(self, name, shape, dt=F32):
        return self.nc.dram_tensor(name, list(shape), dt, kind="ExternalInput").ap()

    def dout(self, name, shape):
        return self.nc.dram_tensor(name, list(shape), F32, kind="ExternalOutput").ap()

    def sb(self, name, shape, dt):
        return self.st.enter_context(self.nc.sbuf_tensor("sb_" + name, list(shape), dt))

    def ua(self, name, shape, dt, off, nb=1):
        t = self.nc.alloc_sbuf_tensor_at("u_" + name, list(shape), dt, offset=self.ubase + off)
        esz = 4 if dt == F32 else 2
        n = 1
        for d in shape[1:]:
            n *= d
        size = n * esz
        assert off + size <= self.usize, (name, off, size)
        per = size // nb
        bufs = [Buf(name, mem=(off + i * per, off + (i + 1) * per)) for i in range(nb)]
        return t, bufs

    def gp(self):
        n = getattr(self, "gn", 7)
        i = self.gi % n
        self.gi = (i + 1) % n
        return self.pt[i], self.pb[i]

    def rot(self, key, n):
        k = self.rc.get(key, 0)
        self.rc[key] = (k + 1) % n
        return k

    def wslot(self):
        i = self.wi
        self.wi = (self.wi + 1) % len(self.wr)
        return self.wr[i], self.wrb[i]

    def build(self):
        nc = self.nc
        A = self.A
        LAT = 1 in self.tiles
        xin = self.din("xin", [2, 1024, 512])
        cond = self.din("cond", [128, 16])
        pvd = self.din("pv", [128, PV_N])
        cmd = self.din("cm", [128, 768])
        w_ada = self.din("w_ada", [NL, 1024, 6144])
        wqkv = self.din("wqkv", [NL, 1024, NQKV])
        wgate = self.din("wgate", [NL, 1024, 8, 512])
        wbr = self.din("wbr", [NL, 8, 128, 1024])
        w_out = self.din("w_out", [NL, 1024, 1024])
        w_up = self.din("w_up", [NL, 1024, 5632])
        w_down = self.din("w_down", [NL, 2816, 1024])
        wukv = self.din("wukv", [NL, 128, 640])
        csd = self.din("cs", [128, 4, 512])
        maskAd = self.din("maskA", [128, 8, 512])
        maskCd = self.din("maskC", [128, 6, 512])
        rseld = self.din("rsel", [15, 2 * NU])
        oh2d = self.din("oh2", [64, 64 * 128])
        rpbd = self.din("rpb", [NL, 4, 15, 31])
        pcfd = self.din("pcf", [128, 2])
        metad = self.nc.dram_tensor("meta", [1, 8], mybir.dt.int32, kind="ExternalInput").ap()
        cka = self.din("cka", [NL, 256, 256])
        cva = self.din("cva", [NL, 256, 256])
        ckb = self.din("ckb", [NL, 128, 256])
        cvb = self.din("cvb", [NL, 256, 128])
        ckc = self.din("ckc", [NL, 128, 256])
        cvc = self.din("cvc", [NL, 256, 128])
        cckv = self.din("cckv", [NL, 128, 256])
        ckpe = self.din("ckpe", [NL, 32, 256])
        y = self.dout("y", [2, 1024, 512])
        ok_a = self.dout("ok_a", [NL, 256, 512])
        ov_a = self.dout("ov_a", [NL, 512, 256])
        ok_b = self.dout("ok_b", [NL, 128, 512])
        ov_b = self.dout("ov_b", [NL, 512, 128])
        ok_c = self.dout("ok_c", [NL, 128, 512])
        ov_c = self.dout("ov_c", [NL, 512, 128])
        o_ckv = self.dout("o_ckv", [NL, 128, 512])
        o_kpe = self.dout("o_kpe", [NL, 32, 512])
        agF_in = nc.dram_tensor("agF_in", [896, 512], BF16).ap()
        agF_out = nc.dram_tensor("agF_out", [4 * 896, 512], BF16).ap()
        KF = nc.dram_tensor("KF", [896, 2048], BF16).ap()
        agT_in = nc.dram_tensor("agT_in", [512, 768], BF16).ap()
        agT_out = nc.dram_tensor("agT_out", [2048, 768], BF16).ap()
        agH_in = nc.dram_tensor("agH_in", [2 * 128, 8], BF16).ap()
        agH_out = nc.dram_tensor("agH_out", [8 * 128, 8], BF16).ap()
        agFin_b, agFout_b, KF_b, agTin_b, agTout_b, agHin_b, agHout_b = [Buf() for _ in range(7)]
        agFin_d, agTin_d = {}, {}
        GROUPS = [[0, 1, 2, 3], [4, 5, 6, 7]]

        sb = self.sb
        xT = [sb("xT%d" % t, [128, 8, 512], F32) for t in range(2)]
        xb = [[Buf() for _ in range(8)] for _ in range(2)]
        hT = [sb("hT%d" % t, [128, 8, 512], BF16) for t in range(2)]
        hb = [[Buf() for _ in range(8)] for _ in range(2)]
        NW = 3
        self.wr = [sb("wr%d" % i, [128, 4096], BF16) for i in range(NW)]
        self.wrb = [Buf() for _ in range(NW)]
        self.wi = 0
        pv = sb("pv", [128, PV_N], F32)
        pvb = Buf()
        cm = sb("cm", [128, 768], BF16)
        cmb = Buf()
        scT = sb("scT", [128, 16], BF16)
        scb = Buf()
        c32 = sb("c32", [128, 16], F32)
        c32b = Buf()
        mods = [sb("mods%d" % i, [128, 96], F32) for i in range(2)]
        modb = [Buf() for _ in range(2)]
        gef = [sb("gef%d" % i, [128, 2, 2, 8], F32) for i in range(2)]
        rinv = sb("rinv", [128, 512], F32)
        rinvb = Buf()
        NT = 4
        tmp = [sb("tmp%d" % i, [128, 512], F32) for i in range(NT)]
        tmpb = [Buf() for _ in range(NT)]
        sq = [sb("sq%d" % i, [128, 512], BF16) for i in range(2)]
        sqb = [Buf() for _ in range(2)]
        ckvn = [sb("ckvn%d" % t, [128, 512], BF16) for t in range(2)]
        ckvb = [Buf(), Buf()]
        kpe = [sb("kpe%d" % t, [32, 512], BF16) for t in range(2)]
        kpeb = [Buf(), Buf()]
        if LAT:
            cs = sb("cs", [128, 4, 512], BF16)
            csb = Buf()
            NKS = 3
            kst = [sb("kst%d" % i, [128, 512], BF16) for i in range(NKS)]
            kstb = [Buf() for _ in range(NKS)]
            rsel = sb("rsel", [15, 2 * NU], BF16)
            rselb = Buf()
            rpbs = sb("rpbs", [15, 4, 32], BF16)
            rpbsb = Buf()
            rpbT2 = sb("rpbT2", [64, 4 * NU], BF16)
            rpbT2b = Buf()
            hh = sb("hh", [128, 2, 8], BF16)
            hhb = Buf()
            pcf = sb("pcf", [128, 2], F32)
            pcfb = Buf()
        USIZE = 102 * KB
        self.usize = USIZE
        ureg = self.st.enter_context(nc.sbuf_tensor("sb_uarena", [128, USIZE], mybir.dt.uint8))
        self.ubase = nc.lookup_mloc(ureg).addr
        ua = self.ua
        Q, Qb, OT, OTb, merged, mgb, act, actb = {}, {}, {}, {}, {}, {}, {}, {}
        o = 0
        for t in range(2):
            for m in "abc":
                Q[t, m], Qb[t, m] = ua("q%s%d" % (m, t), [128, 2, 512], BF16, o, 2)
                o += 2 * KB
            Q[t, "d"], Qb[t, "d"] = ua("qd%d" % t, [128, 4, 512], BF16, o, 4)
            o += 4 * KB
        assert o == 20 * KB
        Kc, Kcb = {}, {}
        Kc["a"], Kcb["a"] = ua("ka", [128, 2, 512], BF16, 20 * KB, 2)
        Kc["b"], Kcb["b"] = ua("kb", [128, 1, 512], BF16, 22 * KB, 1)
        Kc["c"], Kcb["c"] = ua("kc", [128, 1, 512], BF16, 23 * KB, 1)
        Kc["d"], Kcb["d"] = ua("kd", [128, 4, 512], BF16, 24 * KB, 4)
        Vt, vtb_ = ua("Vt", [128, 4, 768], BF16, 28 * KB, 1)
        Vtb = [[Buf(mem=(28 * KB + tb * 1536, 28 * KB + tb * 1536 + 1024)),
                Buf(mem=(28 * KB + tb * 1536 + 1024, 28 * KB + (tb + 1) * 1536))] for tb in range(4)]
        NPT = 5
        PT, PTb = [], []
        for i in range(NPT):
            t_, b_ = ua("PT%d" % i, [128, 512], BF16, 48 * KB + i * KB, 1)
            PT.append(t_)
            PTb.append(b_[0])
        for t in range(2):
            for mi, m in enumerate("abcd"):
                OT[t, m], OTb[t, m] = ua("o%s%d" % (m, t), [128, 2, 512], BF16, 53 * KB + (t * 4 + mi) * 2 * KB, 2)
        gate, gateb = [], []
        for i in range(2):
            t_, b_ = ua("gate%d" % i, [128, 512], BF16, 69 * KB + i * KB, 1)
            gate.append(t_)
            gateb.append(b_[0])
        macc, mb_ = ua("macc", [128, 512], F32, 71 * KB, 1)
        maccb = mb_[0]
        for t in range(2):
            merged[t], mgb[t] = ua("mg%d" % t, [128, 8, 512], BF16, t * 8 * KB, 8)
        if LAT:
            KA, KAb = ua("KA", [128, 2, 1280], BF16, 0, 1)
            VA, VAb = ua("VA", [128, 10, 256], BF16, 5 * KB, 1)
            KD, KDb = ua("KD", [128, 4, 2304], BF16, 20 * KB, 1)
            VD, VDb = ua("VD", [128, 18, 256], BF16, 20 * KB + 18432, 1)
            OH2, OH2b = ua("OH2", [64, 64, 128], BF16, 20 * KB, 1)
            cckvs, cckvsb = ua("cckvs", [128, 256], BF16, 47 * KB, 1)
            ckpes, ckpesb = ua("ckpes", [32, 256], BF16, 47 * KB + 512, 1)
            EBu, EBub = ua("EBu", [128, 4 * NU, 64], BF16, 73 * KB, 1)
            KBa, KBab = ua("KBa", [128, 2304], BF16, 73 * KB, 1)
            VBa, VBab = ua("VBa", [128, 18, 128], BF16, 73 * KB + 4608, 1)
            KCa, KCab = ua("KCa", [128, 1024], BF16, 73 * KB + 9216, 1)
            VCa, VCab = ua("VCa", [128, 8, 128], BF16, 73 * KB + 9216 + 2048, 1)
            assert 73 * KB + 9216 + 4096 <= 88 * KB
            maskA, maskAb = ua("maskA", [128, 8, 512], BF16, 88 * KB, 1)
            maskC, maskCb = ua("maskC", [128, 6, 512], BF16, 96 * KB, 1)
        for t in range(2):
            act[t], actb[t] = ua("act%d" % t, [128, 22, 512], BF16, t * 22 * KB, 22)
        ub, ubb, cacc, caccb = [], [], [], []
        for i in range(4):
            t_, b_ = ua("ub%d" % i, [128, 514], F32, 44 * KB + i * 2080, 1)
            ub.append(t_)
            ubb.append(b_[0])
            t_, b_ = ua("cacc%d" % i, [128, 512], F32, 44 * KB + 8320 + i * 2048, 1)
            cacc.append(t_)
            caccb.append(b_[0])
        self.pt = [self.st.enter_context(nc.psum_tensor("ps%d" % i, [128, 512], F32)) for i in range(8)]
        self.pb = [Buf() for _ in range(8)]
        pt, pb = self.pt, self.pb

        def cmat(i, k=128, m=128):
            return cm[0:k, i * 128:i * 128 + m]

        def sp_init(e):
            if LAT:
                regs = [e.alloc_register("mr%d" % i) for i in range(6)]
                for i, rg in enumerate(regs):
                    e.reg_load(rg, metad[0:1, i:i + 1])
                mx = [8, 10, 1024, 1280, 896, 896]
                self.dyn = [e.snap(rg, min_val=0, max_val=mx[i]) for i, rg in enumerate(regs)]
            return e.dma_start(out=pv[:], in_=pvd)
        A("sp", sp_init, w=[pvb], dma=True)
        A("pool", lambda e: e.dma_start(out=cm[:], in_=cmd), w=[cmb], dma=True)
        A("sp", lambda e: e.dma_start(out=c32[:], in_=cond), w=[c32b], dma=True)
        for t in self.tiles:
            A("sp", lambda e, t=t: e.dma_start(out=xT[t][:], in_=xin[t].rearrange("(c p) n -> p c n", p=128)),
              w=xb[t], dma=True)
        if LAT:
            A("pool", lambda e: e.dma_start(out=cs[:], in_=csd), w=[csb], dma=True)
            A("sp", lambda e: e.dma_start(out=pcf[:], in_=pcfd), w=[pcfb], dma=True)
            A("pool", lambda e: e.dma_start(out=rsel[:], in_=rseld), w=[rselb], dma=True)
            A("pool", lambda e: e.dma_start(out=maskA[:], in_=maskAd), w=maskAb, dma=True)
            A("pool", lambda e: e.dma_start(out=maskC[:], in_=maskCd), w=maskCb, dma=True)
            A("dve", lambda e: e.memset(rpbs[:], 0.0), w=[rpbsb])
        A("act", lambda e: e.activation(out=scT[:], in_=c32[:], func=AF.Silu), r=[c32b], w=[scb])
        for l in range(NL):
            o = l * PV_L
            A("dve", lambda e, l=l, o=o: e.tensor_scalar(out=pv[:, PV_QS + 4 * l:PV_QS + 4 * l + 3],
                                                          in0=pv[:, o + PV_QG:o + PV_QG + 3], scalar1=0.125,
                                                          scalar2=None, op0=ALU.mult), r=[pvb], w=[pvb])
            A("dve", lambda e, l=l, o=o: e.tensor_scalar(out=pv[:, PV_QS + 4 * l + 3:PV_QS + 4 * l + 4],
                                                          in0=pv[:, o + PV_QG + 3:o + PV_QG + 4],
                                                          scalar1=float(96 ** -0.5), scalar2=None, op0=ALU.mult),
              r=[pvb], w=[pvb])
            A("act", lambda e, l=l, o=o: e.activation(out=pv[:, PV_ES + 2 * l:PV_ES + 2 * l + 2],
                                                       in_=pv[:, o + PV_SK:o + PV_SK + 2], func=AF.Exp),
              r=[pvb], w=[pvb])
        eps = pv[:, PV_EPS:PV_EPS + 1]

        def wload(view_src_pairs):
            slot, sbuf_ = self.wslot()
            for k, (dst_fn, src) in enumerate(view_src_pairs):
                A("pool", lambda e, dst_fn=dst_fn, src=src, slot=slot: e.dma_start(out=dst_fn(slot), in_=src),
                  w=[sbuf_], dma=True)
            return slot, sbuf_

        def w3(slot, k, n):
            return slot[:, 0:k * n].rearrange("p (k n) -> p k n", k=k)

        def load_kn(src2d, c0, ncols, k=8):
            return wload([(lambda s: w3(s, k, ncols),
                           src2d.rearrange("(k p) n -> p k n", p=128)[:, :, c0:c0 + ncols])])

        def adaln(l):
            mb = l % 2
            pm, pmb = pt[7], pb[7]
            for g in range(12):
                slot, sbf = load_kn(w_ada[l], g * 512, 512)
                v = w3(slot, 8, 512)
                for j in range(4):
                    col = (g * 4 + j) * 2
                    for kc in range(8):
                        A("pe", lambda e, v=v, j=j, kc=kc, col=col: e.matmul(
                            pm[:, col:col + 2], lhsT=v[:, kc, j * 128:(j + 1) * 128],
                            rhs=scT[:, kc * 2:kc * 2 + 2], start=(kc == 0), stop=(kc == 7)),
                          r=[sbf, scb], w=[pmb])
            m3 = mods[mb][:, 0:96].rearrange("p (a b) -> p a b", b=2)
            p3 = pm[:, 0:96].rearrange("p (a b) -> p a b", b=2)
            o = l * PV_L
            for t in range(2):
                A("dve", lambda e, t=t: e.tensor_tensor(out=m3[:, :, t], in0=p3[:, :, t],
                                                        in1=pv[:, o + PV_B:o + PV_B + 48], op=ALU.add),
                  r=[pmb, pvb], w=[modb[mb]])
            for t in range(2):
                for ni, (sc0, nv) in enumerate(((8, PV_N1), (32, PV_N2))):
                    A("dve", lambda e, t=t, ni=ni, sc0=sc0, nv=nv: e.scalar_tensor_tensor(
                        out=gef[mb][:, ni, t, :], in0=m3[:, sc0:sc0 + 8, t], scalar=1.0,
                        in1=pv[:, o + nv:o + nv + 8], op0=ALU.add, op1=ALU.mult),
                      r=[modb[mb], pvb], w=[modb[mb]])

        def mod(l, t, idx, c):
            col = ((idx * 8 + c) * 2 + t)
            return mods[l % 2][:, col:col + 1]

        def norm_h(l, t, ni):
            ps, psb = self.gp()
            for c in range(8):
                k = self.rot("sq", 2)
                A("act", lambda e, c=c, k=k: e.activation(out=sq[k][:], in_=xT[t][:, c, :], func=AF.Square),
                  r=[xb[t][c]], w=[sqb[k]])
                A("pe", lambda e, c=c, k=k: e.matmul(ps[:], lhsT=cmat(CM_ONES), rhs=sq[k][:],
                                                     start=(c == 0), stop=(c == 7)),
                  r=[sqb[k], cmb], w=[psb])
            A("act", lambda e: e.activation(out=rinv[:], in_=ps[:], func=AF.Ln, scale=1.0 / 1024, bias=eps),
              r=[psb, pvb], w=[rinvb])
            A("act", lambda e: e.activation(out=rinv[:], in_=rinv[:], func=AF.Exp, scale=-0.5), r=[rinvb], w=[rinvb])
            for c in range(8):
                k = self.rot("tmp", NT)
                A("dve", lambda e, c=c, k=k: e.tensor_tensor(out=tmp[k][:], in0=xT[t][:, c, :], in1=rinv[:],
                                                             op=ALU.mult),
                  r=[xb[t][c], rinvb], w=[tmpb[k]])
                A("act", lambda e, c=c, k=k: e.activation(
                    out=hT[t][:, c, :], in_=tmp[k][:], func=AF.Identity,
                    scale=gef[l % 2][:, ni, t, c:c + 1], bias=mod(l, t, 3 * ni, c)),
                  r=[tmpb[k], modb[l % 2]], w=[hb[t][c]])

        def proj_fm(v, sbf, j0, M, t):
            ps, psb = self.gp()
            for kc in range(8):
                A("pe", lambda e, kc=kc: e.matmul(ps[0:M, :], lhsT=v[:, kc, j0:j0 + M], rhs=hT[t][:, kc, :],
                                                  start=(kc == 0), stop=(kc == 7)),
                  r=[sbf, hb[t][kc]], w=[psb])
            return ps, psb

        def headnorm(ps, psb, M, cmi, invn, gain, dst, dstb, want32=False, N=512):
            k = self.rot("sq", 2)
            A("act", lambda e: e.activation(out=sq[k][0:M, 0:N], in_=ps[0:M, 0:N], func=AF.Square),
              r=[psb], w=[sqb[k]])
            p2, p2b = self.gp()
            A("pe", lambda e: e.matmul(p2[0:M, 0:N], lhsT=cmat(cmi, M, M), rhs=sq[k][0:M, 0:N], start=True, stop=True),
              r=[sqb[k], cmb], w=[p2b])
            k1 = self.rot("tmp", NT)
            A("act", lambda e: e.activation(out=tmp[k1][0:M, 0:N], in_=p2[0:M, 0:N], func=AF.Ln, scale=invn,
                                            bias=eps[0:M]), r=[p2b, pvb], w=[tmpb[k1]])
            A("act", lambda e: e.activation(out=tmp[k1][0:M, 0:N], in_=tmp[k1][0:M, 0:N], func=AF.Exp, scale=-0.5),
              r=[tmpb[k1]], w=[tmpb[k1]])
            if want32:
                k2 = self.rot("tmp", NT)
                A("dve", lambda e: e.scalar_tensor_tensor(out=tmp[k2][0:M, 0:N], in0=ps[0:M, 0:N], scalar=gain[0:M],
                                                          in1=tmp[k1][0:M, 0:N], op0=ALU.mult, op1=ALU.mult),
                  r=[psb, tmpb[k1], pvb], w=[tmpb[k2]])
                A("act", lambda e: e.activation(out=dst, in_=tmp[k2][0:M, 0:N], func=AF.Identity), r=[tmpb[k2]], w=[dstb])
                return k2
            A("dve", lambda e: e.scalar_tensor_tensor(out=dst, in0=ps[0:M, 0:N], scalar=gain[0:M],
                                                      in1=tmp[k1][0:M, 0:N], op0=ALU.mult, op1=ALU.mult),
              r=[psb, tmpb[k1], pvb], w=[dstb])
            return None

        def store(dst, k, M):
            self.outs.append(A("sp", lambda e: e.dma_start(out=dst, in_=tmp[k][0:M, :]), r=[tmpb[k]], dma=True))

        def rope(src, srcb, M, pidx, ci, dst, dstb):
            p3_, p3b = self.gp()
            A("pe", lambda e: e.matmul(p3_[0:M, :], lhsT=cmat(pidx, M, M), rhs=src, start=True, stop=True),
              r=[srcb, cmb], w=[p3b])
            k1 = self.rot("tmp", NT)
            A("dve", lambda e: e.tensor_tensor(out=tmp[k1][0:M, :], in0=src, in1=cs[0:M, ci, :], op=ALU.mult),
              r=[srcb, csb], w=[tmpb[k1]])
            k2 = self.rot("tmp", NT)
            A("dve", lambda e: e.tensor_tensor(out=tmp[k2][0:M, :], in0=p3_[0:M, :], in1=cs[0:M, ci + 1, :],
                                               op=ALU.mult), r=[p3b, csb], w=[tmpb[k2]])
            A("dve", lambda e: e.tensor_tensor(out=dst, in0=tmp[k1][0:M, :], in1=tmp[k2][0:M, :], op=ALU.add),
              r=[tmpb[k1], tmpb[k2]], w=[dstb])

        def lat_k(ps, psb, M, cmi, invn, gain, do_rope, pidx, ci, row0):
            k = self.rot("kst", NKS)
            headnorm(ps, psb, M, cmi, invn, gain, kst[k][0:M, :], kstb[k])
            if do_rope:
                k2 = self.rot("kst", NKS)
                rope(kst[k][0:M, :], kstb[k], M, pidx, ci, kst[k2][0:M, :], kstb[k2])
                k = k2
            A("sp", lambda e: e.dma_start(out=agF_in[row0:row0 + M, :], in_=kst[k][0:M, :]),
              r=[kstb[k]], w=[agFin_d.setdefault(row0, Buf())], dma=True)

        def qkv(l, t, grp, v, sbf):
            o = l * PV_L
            qs = lambda i: pv[:, PV_QS + 4 * l + i:PV_QS + 4 * l + i + 1]
            kg = lambda i: pv[:, o + PV_KG + i:o + PV_KG + i + 1]
            lat = (t == 1)
            if grp == 0:
                for j in range(2):
                    ps, psb = proj_fm(v, sbf, j * 128, 128, t)
                    headnorm(ps, psb, 128, CM_BD64, 1 / 64., qs(0), Q[t, "a"][:, j, :], [Qb[t, "a"][j]][0])
                for j in range(2):
                    ps, psb = proj_fm(v, sbf, 256 + j * 128, 128, t)
                    if lat:
                        lat_k(ps, psb, 128, CM_BD64, 1 / 64., kg(0), False, 0, 0, j * 128)
                    else:
                        k2 = headnorm(ps, psb, 128, CM_BD64, 1 / 64., kg(0), Kc["a"][:, j, :], Kcb["a"][j], want32=True)
                        store(ok_a[l, j * 128:(j + 1) * 128, :], k2, 128)
            elif grp == 1:
                for j in range(2):
                    ps, psb = proj_fm(v, sbf, j * 128, 128, t)
                    headnorm(ps, psb, 128, CM_BD64, 1 / 64., qs(1), Q[t, "b"][:, j, :], Qb[t, "b"][j])
                    if lat:
                        rope(Q[t, "b"][:, j, :], Qb[t, "b"][j], 128, CM_P64, 0, Q[t, "b"][:, j, :], Qb[t, "b"][j])
                for j, (m, okd) in enumerate((("b", ok_b), ("c", ok_c))):
                    ps, psb = proj_fm(v, sbf, 256 + j * 128, 128, t)
                    if lat:
                        lat_k(ps, psb, 128, CM_BD64, 1 / 64., kg(1 + j), True, CM_P64, 0, 256 + j * 128)
                    else:
                        k2 = headnorm(ps, psb, 128, CM_BD64, 1 / 64., kg(1 + j), Kc[m][:, 0, :], Kcb[m][0], want32=True)
                        store(okd[l], k2, 128)
            elif grp == 2:
                for j in range(2):
                    ps, psb = proj_fm(v, sbf, j * 128, 128, t)
                    headnorm(ps, psb, 128, CM_BD64, 1 / 64., qs(2), Q[t, "c"][:, j, :], Qb[t, "c"][j])
                    if lat:
                        rope(Q[t, "c"][:, j, :], Qb[t, "c"][j], 128, CM_P64, 0, Q[t, "c"][:, j, :], Qb[t, "c"][j])
                ps, psb = proj_fm(v, sbf, 256, 128, t)
                k2 = headnorm(ps, psb, 128, CM_ONES, 1 / 128., kg(4), ckvn[t][:], ckvb[t], want32=True)
                if not lat:
                    store(o_ckv[l], k2, 128)
                ps, psb = proj_fm(v, sbf, 384, 32, t)
                k3 = self.rot("tmp", NT)
                A("act", lambda e: e.activation(out=tmp[k3][0:32, :], in_=ps[0:32, :], func=AF.Identity),
                  r=[psb], w=[tmpb[k3]])
                A("act", lambda e: e.activation(out=kpe[t][:], in_=tmp[k3][0:32, :], func=AF.Identity), r=[tmpb[k3]], w=[kpeb[t]])
                if not lat:
                    store(o_kpe[l], k3, 32)
            elif grp == 3:
                for h in range(4):
                    ps, psb = proj_fm(v, sbf, h * 128, 96, t)
                    headnorm(ps, psb, 96, CM_ONES96, 1 / 96., qs(3), Q[t, "d"][0:96, h, :], Qb[t, "d"][h])
                    if lat:
                        rope(Q[t, "d"][0:96, h, :], Qb[t, "d"][h], 96, CM_P96, 2, Q[t, "d"][0:96, h, :], Qb[t, "d"][h])
            elif grp == 4:
                for tb in range(4):
                    ps, psb = self.gp()
                    for kc in range(8):
                        A("pe", lambda e, kc=kc, ps=ps, tb=tb: e.matmul(
                            ps[:], lhsT=hT[t][:, kc, tb * 128:(tb + 1) * 128], rhs=v[:, kc, :],
                            start=(kc == 0), stop=(kc == 7)), r=[sbf, hb[t][kc]], w=[psb])
                    if lat:
                        k = self.rot("kst", NKS)
                        A("act", lambda e, k=k, ps=ps: e.activation(out=kst[k][:], in_=ps[:], func=AF.Identity),
                          r=[psb], w=[kstb[k]])
                        A("sp", lambda e, k=k, tb=tb: e.dma_start(out=agT_in[tb * 128:(tb + 1) * 128, 0:512],
                                                                   in_=kst[k][:]), r=[kstb[k]], w=[agTin_d.setdefault((tb, 0), Buf())], dma=True)
                        continue
                    k = self.rot("tmp", NT)
                    A("act", lambda e, k=k, ps=ps: e.activation(out=tmp[k][:], in_=ps[:], func=AF.Identity),
                      r=[psb], w=[tmpb[k]])
                    A("act", lambda e, k=k, tb=tb: e.activation(out=Vt[:, tb, 0:512], in_=tmp[k][:], func=AF.Identity),
                      r=[tmpb[k]], w=[Vtb[tb][0]])
                    for dst, c0, n in ((ov_a, 0, 256), (ov_b, 256, 128), (ov_c, 384, 128)):
                        self.outs.append(A("sp", lambda e, k=k, dst=dst, c0=c0, n=n, tb=tb: e.dma_start(
                            out=dst[l, tb * 128:(tb + 1) * 128, :], in_=tmp[k][:, c0:c0 + n]), r=[tmpb[k]], dma=True))

        def mla_load(l):
            slot, sbf = wload([(lambda s: s[:, 0:640], wukv[l])])
            wk = slot[:, 0:384].rearrange("p (h n) -> p h n", h=4)
            wv = slot[:, 384:640]
            return wk, wv, sbf

        def mla_kraw(wk, sbf, h, ck, ckb_, kp, kpb_, N):
            ps, psb = self.gp()
            A("pe", lambda e: e.matmul(ps[0:96, 0:N], lhsT=wk[:, h, :], rhs=ck, start=True, stop=False),
              r=[sbf, ckb_], w=[psb])
            A("pe", lambda e: e.matmul(ps[0:96, 0:N], lhsT=cm[0:32, CM_SEL * 128:CM_SEL * 128 + 96], rhs=kp,
                                       start=False, stop=True), r=[cmb, kpb_], w=[psb])
            return ps, psb

        def mla_kv(l, wk, wv, sbf):
            o = l * PV_L
            kdg = pv[:, o + PV_KG + 3:o + PV_KG + 4]
            for t in self.tiles:
                for h in range(4):
                    ps, psb = mla_kraw(wk, sbf, h, ckvn[t][:], ckvb[t], kpe[t][:], kpeb[t], 512)
                    if t == 0:
                        headnorm(ps, psb, 96, CM_ONES96, 1 / 96., kdg, Kc["d"][0:96, h, :], Kcb["d"][h])
                    else:
                        lat_k(ps, psb, 96, CM_ONES96, 1 / 96., kdg, True, CM_P96, 2, 512 + 96 * h)
                for tb in range(4):
                    ps, psb = self.gp()
                    A("pe", lambda e, tb=tb, ps=ps, t=t: e.matmul(ps[:, 0:256], lhsT=ckvn[t][:, tb * 128:(tb + 1) * 128],
                                                                 rhs=wv, start=True, stop=True),
                      r=[sbf, ckvb[t]], w=[psb])
                    if t == 0:
                        A("act", lambda e, tb=tb, ps=ps: e.activation(out=Vt[:, tb, 512:768], in_=ps[:, 0:256],
                                                                      func=AF.Identity), r=[psb], w=[Vtb[tb][1]])
                    else:
                        k = self.rot("kst", NKS)
                        A("act", lambda e, k=k, ps=ps: e.activation(out=kst[k][:, 0:256], in_=ps[:, 0:256],
                                                                    func=AF.Identity), r=[psb], w=[kstb[k]])
                        A("sp", lambda e, k=k, tb=tb: e.dma_start(out=agT_in[tb * 128:(tb + 1) * 128, 512:768],
                                                                   in_=kst[k][:, 0:256]), r=[kstb[k]], w=[agTin_d.setdefault((tb, 1), Buf())],
                          dma=True)

        def attn_chunk(t, m, ch, heads, l, sink=False):
            ai = self.rot("acc", 2)
            po, pob = pt[4 + ai], pb[4 + ai]
            pl, plb = pt[6 + ai], pb[6 + ai]
            seq = []
            for half, hd in enumerate(heads):
                first = {}
                for tl in hd["tiles"]:
                    grp = tl[7]
                    st_ = grp not in first
                    first[grp] = True
                    seq.append((half, hd, tl, st_, hd["last"][grp] is tl[0]))
            def stage1(i):
                half, hd, (kv, kbuf, vv, vbuf, q0, q1, post, grp), st_, last = seq[i]
                qv, qbuf = hd["q"]
                n = q1 - q0
                ps, psb = self.gp()
                A("pe", lambda e: e.matmul(ps[:, 0:n], lhsT=kv, rhs=qv[:, q0:q1], start=True, stop=True),
                  r=[kbuf, qbuf], w=[psb])
                k = self.rot("PT", NPT)
                A("act", lambda e: e.activation(out=PT[k][:, 0:n], in_=ps[:, 0:n], func=AF.Exp), r=[psb], w=[PTb[k]])
                if post is not None:
                    for (mk, mkb) in post:
                        A("dve", lambda e, mk=mk: e.tensor_tensor(out=PT[k][:, 0:n], in0=PT[k][:, 0:n], in1=mk,
                                                                  op=ALU.mult), r=[PTb[k]] + mkb, w=[PTb[k]])
                return k

            def stage2(i, k):
                half, hd, (kv, kbuf, vv, vbuf, q0, q1, post, grp), st_, last = seq[i]
                n = q1 - q0
                osl = slice(half * 64, half * 64 + 64)
                A("pe", lambda e: e.matmul(po[osl, q0:q1], lhsT=vv, rhs=PT[k][:, 0:n], start=st_, stop=last),
                  r=[PTb[k], vbuf], w=[pob])
                A("pe", lambda e: e.matmul(pl[osl, q0:q1], lhsT=cm[:, 0:64], rhs=PT[k][:, 0:n], start=st_, stop=last),
                  r=[PTb[k], cmb], w=[plb])

            def finalize():
                k1 = self.rot("tmp", NT)
                if sink:
                    A("act", lambda e: e.activation(out=tmp[k1][:], in_=pl[:], func=AF.Ln,
                                                    bias=pv[:, PV_ES + 2 * l + ch:PV_ES + 2 * l + ch + 1]),
                      r=[plb, pvb], w=[tmpb[k1]])
                else:
                    A("act", lambda e: e.activation(out=tmp[k1][:], in_=pl[:], func=AF.Ln), r=[plb], w=[tmpb[k1]])
                A("act", lambda e: e.activation(out=tmp[k1][:], in_=tmp[k1][:], func=AF.Exp, scale=-1.0),
                  r=[tmpb[k1]], w=[tmpb[k1]])
                A("dve", lambda e: e.tensor_tensor(out=OT[t, m][:, ch, :], in0=po[:], in1=tmp[k1][:], op=ALU.mult),
                  r=[pob, tmpb[k1]], w=[OTb[t, m][ch]])
            for i in range(len(seq)):
                cell = {}

                def s1(i=i, cell=cell):
                    cell["k"] = stage1(i)

                def s2(i=i, cell=cell, fin=(i == len(seq) - 1)):
                    stage2(i, cell["k"])
                    if fin:
                        finalize()
                s1()
                self.pipe.append(s2)
                if len(self.pipe) > 3:
                    self.pipe.pop(0)()

        def pipe_drain():
            while self.pipe:
                self.pipe.pop(0)()

        def ctx_attn(l):
            t = 0
            for m in "abcd":
                for ch in range(2):
                    heads = []
                    for half in range(2):
                        if m == "d":
                            hd = 2 * ch + half
                            q = (Q[t, "d"][0:96, hd, :], Qb[t, "d"][hd])
                            kfull, kbuf = Kc["d"][0:96, hd, :], Kcb["d"][hd]
                            vcol, vsel = 512 + hd * 64, 1
                        else:
                            q = (Q[t, m][half * 64:half * 64 + 64, ch, :], Qb[t, m][ch])
                            if m == "a":
                                kfull, kbuf = Kc["a"][half * 64:half * 64 + 64, ch, :], Kcb["a"][ch]
                                vcol = (2 * ch + half) * 64
                            else:
                                kfull, kbuf = Kc[m][half * 64:half * 64 + 64, 0, :], Kcb[m][0]
                                vcol = (256 if m == "b" else 384) + half * 64
                            vsel = 0
                        tiles = []
                        last = {}
                        for b in range(2):
                            for kt in range(2):
                                tb = 2 * b + kt
                                kv = kfull[:, tb * 128:(tb + 1) * 128]
                                tiles.append((kv, kbuf, Vt[:, tb, vcol:vcol + 64], Vtb[tb][vsel],
                                              b * 256, (b + 1) * 256, None, b))
                                last[b] = kv
                        heads.append(dict(q=q, tiles=tiles, last=last))
                    attn_chunk(0, m, ch, heads, l, sink=(m == "c"))

        def gather(l):
            A("pool", lambda e: e.collective_compute("AllGather", ALU.bypass, replica_groups=GROUPS,
                                                     ins=[agF_in], outs=[agF_out]),
              r=list(agFin_d.values()), w=[agFout_b], dma=True, inc=1, csem="F")
            A("pool", lambda e: e.collective_compute("AllGather", ALU.bypass, replica_groups=GROUPS,
                                                     ins=[agT_in], outs=[agT_out]),
              r=list(agTin_d.values()), w=[agTout_b], dma=True, inc=1, csem="T")
            for q in range(4):
                A("sp", lambda e, q=q: e.dma_start(out=KF[:, 512 * q:512 * (q + 1)],
                                                   in_=agF_out[896 * q:896 * (q + 1), :]),
                  r=[agFout_b], w=[KF_b], dma=True)

        agT_v = agT_out.rearrange("(t p) c -> p t c", p=128)

        def ebu_build(l):
            A("pool", lambda e: e.dma_start(out=rpbs[:, :, 0:31], in_=rpbd[l].rearrange("h r c -> r h c")),
              w=[rpbsb], dma=True)
            A("pool", lambda e: e.dma_start(out=OH2[:].rearrange("p a b -> p (a b)"), in_=oh2d), w=OH2b, dma=True)
            pr, prb = self.gp()
            for h in range(4):
                for hf in range(2):
                    A("pe", lambda e, h=h, hf=hf: e.matmul(
                        pr[hf * 32:hf * 32 + 32, h * NU:(h + 1) * NU], lhsT=rpbs[:, h, :],
                        rhs=rsel[:, hf * NU:(hf + 1) * NU], start=True, stop=True),
                      r=[rpbsb, rselb], w=[prb])
            A("act", lambda e: e.activation(out=rpbT2[:], in_=pr[0:64, 0:4 * NU], func=AF.Identity),
              r=[prb], w=[rpbT2b])
            for g in range(16):
                pe_, peb = self.gp()
                for qq in range(4):
                    qc = g * 4 + qq
                    A("pe", lambda e, qc=qc, qq=qq, pe_=pe_: e.matmul(
                        pe_[:, qq * 4 * NU:(qq + 1) * 4 * NU], lhsT=OH2[:, qc, :], rhs=rpbT2[:], start=True, stop=True),
                      r=OH2b + [rpbT2b], w=[peb])
                A("act", lambda e, g=g, pe_=pe_: e.activation(
                    out=EBu[:, :, g * 4:(g + 1) * 4],
                    in_=pe_[:, 0:16 * NU].rearrange("p (q n) -> p n q", q=4), func=AF.Exp),
                  r=[peb], w=EBub)

        def lat_attn(l):
            t = 1
            d = lambda i: self.dyn[i]
            A("pool", lambda e: e.dma_start(out=KA[:, :, 0:256], in_=cka[l].rearrange("(c p) n -> p c n", p=128)),
              w=KAb, dma=True)
            A("pool", lambda e: e.dma_start(out=VA[:, 0:2, :], in_=cva[l].rearrange("(t p) c -> p t c", p=128)),
              w=VAb, dma=True)
            for c in range(2):
                A("sp", lambda e, c=c: e.dma_start(out=KA[:, c, 256:1280],
                                                   in_=KF[c * 128:(c + 1) * 128, bass.ds(d(2), 1024)]),
                  r=[KF_b], w=KAb, dma=True)
            A("sp", lambda e: e.dma_start(out=VA[:, 2:10, :], in_=agT_v[:, bass.ds(d(0), 8), 0:256]),
              r=[agTout_b], w=VAb, dma=True)
            for ch in range(2):
                heads = []
                for half in range(2):
                    h = 2 * ch + half
                    q = (Q[t, "a"][half * 64:half * 64 + 64, ch, :], Qb[t, "a"][ch])
                    tiles = []
                    for kt in range(10):
                        kv = KA[half * 64:half * 64 + 64, ch, kt * 128:(kt + 1) * 128]
                        post = None
                        if kt >= 2:
                            u0 = 14 - 2 * (kt - 2)
                            post = [(EBu[:, h * NU + u0:h * NU + u0 + 8, :].rearrange("p a b -> p (a b)"), EBub),
                                    (maskA[:, kt - 2, :], maskAb)]
                        tiles.append((kv, KAb[0], VA[:, kt, h * 64:(h + 1) * 64], VAb[0], 0, 512, post, 0))
                    heads.append(dict(q=q, tiles=tiles, last={0: tiles[-1][0]}))
                attn_chunk(1, "a", ch, heads, l)
            A("pool", lambda e: e.dma_start(out=KCa[:, 0:256], in_=ckc[l]), w=KCab, dma=True)
            A("pool", lambda e: e.dma_start(out=VCa[:, 0:2, :], in_=cvc[l].rearrange("(t p) c -> p t c", p=128)),
              w=VCab, dma=True)
            A("sp", lambda e: e.dma_start(out=KCa[:, 256:1024], in_=KF[384:512, bass.ds(d(3), 768)]),
              r=[KF_b], w=KCab, dma=True)
            A("sp", lambda e: e.dma_start(out=VCa[:, 2:8, :], in_=agT_v[:, bass.ds(d(1), 6), 384:512]),
              r=[agTout_b], w=VCab, dma=True)
            for ch in range(2):
                heads = []
                for half in range(2):
                    q = (Q[t, "c"][half * 64:half * 64 + 64, ch, :], Qb[t, "c"][ch])
                    tiles = []
                    for kt in range(8):
                        kv = KCa[half * 64:half * 64 + 64, kt * 128:(kt + 1) * 128]
                        post = [(maskC[:, kt - 2, :], maskCb)] if kt >= 2 else None
                        tiles.append((kv, KCab[0], VCa[:, kt, half * 64:(half + 1) * 64], VCab[0], 0, 512, post, 0))
                    heads.append(dict(q=q, tiles=tiles, last={0: tiles[-1][0]}))
                attn_chunk(1, "c", ch, heads, l, sink=True)
            A("pool", lambda e: e.dma_start(out=KBa[:, 0:256], in_=ckb[l]), w=KBab, dma=True)
            A("pool", lambda e: e.dma_start(out=VBa[:, 0:2, :], in_=cvb[l].rearrange("(t p) c -> p t c", p=128)),
              w=VBab, dma=True)
            A("sp", lambda e: e.dma_start(out=KBa[:, 256:2304], in_=KF[256:384, :]), r=[KF_b], w=KBab, dma=True)
            A("sp", lambda e: e.dma_start(out=VBa[:, 2:18, :], in_=agT_v[:, :, 256:384]), r=[agTout_b], w=VBab,
              dma=True)
            for ch in range(2):
                heads = []
                for half in range(2):
                    q = (Q[t, "b"][half * 64:half * 64 + 64, ch, :], Qb[t, "b"][ch])
                    tiles = []
                    for kt in range(18):
                        kv = KBa[half * 64:half * 64 + 64, kt * 128:(kt + 1) * 128]
                        tiles.append((kv, KBab[0], VBa[:, kt, half * 64:(half + 1) * 64], VBab[0], 0, 512, None, 0))
                    heads.append(dict(q=q, tiles=tiles, last={0: tiles[-1][0]}))
                attn_chunk(1, "b", ch, heads, l)
            o = l * PV_L
            kdg = pv[:, o + PV_KG + 3:o + PV_KG + 4]
            wk, wv, sbf = mla_load(l)
            A("pool", lambda e: e.dma_start(out=cckvs[:], in_=cckv[l]), w=cckvsb, dma=True)
            A("pool", lambda e: e.dma_start(out=ckpes[:], in_=ckpe[l]), w=ckpesb, dma=True)
            for h in range(4):
                ps, psb = mla_kraw(wk, sbf, h, cckvs[:], cckvsb[0], ckpes[:], ckpesb[0], 256)
                headnorm(ps, psb, 96, CM_ONES96, 1 / 96., kdg, KD[0:96, h, 0:256], KDb[0], N=256)
            for tb in range(2):
                ps, psb = self.gp()
                A("pe", lambda e, tb=tb, ps=ps: e.matmul(ps[:, 0:256], lhsT=cckvs[:, tb * 128:(tb + 1) * 128], rhs=wv,
                                                         start=True, stop=True), r=[sbf, cckvsb[0]], w=[psb])
                A("act", lambda e, tb=tb, ps=ps: e.activation(out=VD[:, tb, :], in_=ps[:, 0:256], func=AF.Identity),
                  r=[psb], w=VDb)
            for h in range(4):
                A("sp", lambda e, h=h: e.dma_start(out=KD[0:96, h, 256:2304], in_=KF[512 + 96 * h:512 + 96 * (h + 1), :]),
                  r=[KF_b], w=KDb, dma=True)
            A("sp", lambda e: e.dma_start(out=VD[:, 2:18, :], in_=agT_v[:, :, 512:768]), r=[agTout_b], w=VDb, dma=True)
            for ch in range(2):
                heads = []
                for half in range(2):
                    h = 2 * ch + half
                    q = (Q[t, "d"][0:96, h, :], Qb[t, "d"][h])
                    tiles = []
                    for kt in range(18):
                        kv = KD[0:96, h, kt * 128:(kt + 1) * 128]
                        tiles.append((kv, KDb[0], VD[:, kt, h * 64:(h + 1) * 64], VDb[0], 0, 512, None, 0))
                    heads.append(dict(q=q, tiles=tiles, last={0: tiles[-1][0]}))
                attn_chunk(1, "d", ch, heads, l)

        def merge(l):
            for c in range(8):
                sa, sab = wload([(lambda s: w3(s, 8, 512),
                                  wgate[l].rearrange("(k p) c n -> p k c n", p=128)[:, :, c, :])])
                va = w3(sa, 8, 512)
                sbr, sbrb = wload([(lambda s: s[:, 0:1024], wbr[l, c])])
                vb = sbr[:, 0:1024].rearrange("p (n k j) -> p n k j", n=4, k=2)
                for t in self.tiles:
                    for n, m in enumerate("abcd"):
                        pg, pgb = proj_fm(va, sab, n * 128, 128, t)
                        gk = self.rot("gate", 2)
                        A("act", lambda e, gk=gk, pg=pg: e.activation(out=gate[gk][:], in_=pg[:], func=AF.Sigmoid),
                          r=[pgb], w=[gateb[gk]])
                        pp, ppb = self.gp()
                        for k2 in range(2):
                            A("pe", lambda e, n=n, k2=k2, pp=pp, m=m, t=t, vb=vb: e.matmul(
                                pp[:], lhsT=vb[:, n, k2, :], rhs=OT[t, m][:, k2, :], start=(k2 == 0), stop=(k2 == 1)),
                              r=[sbrb, OTb[t, m][k2]], w=[ppb])
                        if n == 0:
                            A("dve", lambda e, pp=pp, gk=gk: e.tensor_tensor(out=macc[:], in0=pp[:], in1=gate[gk][:],
                                                                             op=ALU.mult),
                              r=[ppb, gateb[gk]], w=[maccb])
                        else:
                            k = self.rot("tmp", NT)
                            A("dve", lambda e, pp=pp, gk=gk, k=k: e.tensor_tensor(out=tmp[k][:], in0=pp[:],
                                                                                  in1=gate[gk][:], op=ALU.mult),
                              r=[ppb, gateb[gk]], w=[tmpb[k]])
                            if n < 3:
                                A("dve", lambda e, k=k: e.tensor_tensor(out=macc[:], in0=macc[:], in1=tmp[k][:],
                                                                         op=ALU.add),
                                  r=[maccb, tmpb[k]], w=[maccb])
                            else:
                                A("dve", lambda e, k=k, t=t, c=c: e.tensor_tensor(out=merged[t][:, c, :], in0=macc[:],
                                                                                   in1=tmp[k][:], op=ALU.add),
                                  r=[maccb, tmpb[k]], w=[mgb[t][c]])
            for g in range(2):
                slot, sbf = load_kn(w_out[l], g * 512, 512)
                v = w3(slot, 8, 512)
                for j in range(4):
                    c = g * 4 + j
                    for t in self.tiles:
                        ps, psb = self.gp()
                        for kc in range(8):
                            A("pe", lambda e, kc=kc, ps=ps, t=t, j=j, v=v: e.matmul(
                                ps[:], lhsT=v[:, kc, j * 128:(j + 1) * 128], rhs=merged[t][:, kc, :],
                                start=(kc == 0), stop=(kc == 7)), r=[sbf, mgb[t][kc]], w=[psb])
                        A("dve", lambda e, ps=ps, t=t, c=c: e.scalar_tensor_tensor(
                            out=xT[t][:, c, :], in0=ps[:], scalar=mod(l, t, 2, c), in1=xT[t][:, c, :],
                            op0=ALU.mult, op1=ALU.add), r=[psb, xb[t][c], modb[l % 2]], w=[xb[t][c]])

        def halo(l):
            agH_v = agH_in.rearrange("(w p) k -> w p k", p=128)
            for w_, col in ((0, 0), (1, 511)):
                A("sp", lambda e, w_=w_, col=col: e.dma_start(out=agH_v[w_], in_=hT[1][:, :, col], allow_slow_non_contiguous=True),
                  r=hb[1], w=[agHin_b], dma=True)
            A("pool", lambda e: e.collective_compute("AllGather", ALU.bypass, replica_groups=GROUPS,
                                                     ins=[agH_in], outs=[agH_out]),
              r=[agHin_b], w=[agHout_b], dma=True, inc=1, csem="H")
            for j, di in ((0, 4), (1, 5)):
                A("sp", lambda e, j=j, di=di: e.dma_start(out=hh[:, j, :], in_=agH_out[bass.ds(self.dyn[di], 128), :]),
                  r=[agHout_b], w=[hhb], dma=True)
            for j in range(2):
                A("dve", lambda e, j=j: e.tensor_scalar(out=hh[:, j, :], in0=hh[:, j, :], scalar1=pcf[:, j:j + 1],
                                                        scalar2=None, op0=ALU.mult), r=[hhb, pcfb], w=[hhb])

        def ffn(l):
            o = l * PV_L
            cw = lambda j, ch: pv[:, o + PV_CW + j * 44 + ch:o + PV_CW + j * 44 + ch + 1]
            cb = lambda ch: pv[:, o + PV_CB + ch:o + PV_CB + ch + 1]
            for j in range(11):
                slot, sbf = wload([(lambda s: w3(s, 8, 512)[:, :, 0:256],
                                    w_up[l].rearrange("(k p) n -> p k n", p=128)[:, :, j * 256:(j + 1) * 256]),
                                   (lambda s: w3(s, 8, 512)[:, :, 256:512],
                                    w_up[l].rearrange("(k p) n -> p k n", p=128)[:, :, 2816 + j * 256:2816 + (j + 1) * 256])])
                v = w3(slot, 8, 512)
                for t in self.tiles:
                    for q in range(2):
                        ch = 2 * j + q
                        par = 2 * self.rot("ubpar", 2)
                        for part0 in range(2):
                            part = part0 + par
                            cch = ch + 22 * part0
                            c0 = part0 * 256 + q * 128
                            ps, psb = proj_fm(v, sbf, c0, 128, t)
                            A("act", lambda e, ps=ps, part=part: e.activation(out=ub[part][:, 1:513], in_=ps[:],
                                                                             func=AF.Identity),
                              r=[psb], w=[ubb[part]])
                            if t == 1:
                                ph_, phb = self.gp()
                                for kc in range(8):
                                    A("pe", lambda e, kc=kc, ph_=ph_, c0=c0, v=v: e.matmul(
                                        ph_[:, 0:2], lhsT=v[:, kc, c0:c0 + 128], rhs=hh[:, :, kc],
                                        start=(kc == 0), stop=(kc == 7)), r=[sbf, hhb], w=[phb])
                                A("act", lambda e, ph_=ph_, part=part: e.activation(
                                    out=ub[part][:, 0:1], in_=ph_[:, 0:1], func=AF.Identity),
                                  r=[phb, ubb[part]], w=[ubb[part]])
                                A("act", lambda e, ph_=ph_, part=part: e.activation(
                                    out=ub[part][:, 513:514], in_=ph_[:, 1:2], func=AF.Identity),
                                  r=[phb, ubb[part]], w=[ubb[part]])
                            A("act", lambda e, ps=ps, part=part, cch=cch: e.activation(
                                out=cacc[part][:], in_=ps[:], func=AF.Identity, scale=cw(1, cch), bias=cb(cch)),
                              r=[psb, pvb], w=[caccb[part]])
                            if t == 0:
                                segs0 = ((1, 256, 1, 256), (257, 512, 257, 512))
                                segs2 = ((0, 255, 2, 257), (256, 511, 258, 513))
                            else:
                                segs0 = ((0, 512, 0, 512),)
                                segs2 = ((0, 512, 2, 514),)
                            for (a0, a1, u0, u1) in segs0:
                                A("dve", lambda e, part=part, cch=cch, a0=a0, a1=a1, u0=u0, u1=u1: e.scalar_tensor_tensor(
                                    out=cacc[part][:, a0:a1], in0=ub[part][:, u0:u1], scalar=cw(0, cch),
                                    in1=cacc[part][:, a0:a1], op0=ALU.mult, op1=ALU.add),
                                  r=[ubb[part], caccb[part], pvb], w=[caccb[part]])
                            for (a0, a1, u0, u1) in segs2:
                                A("dve", lambda e, part=part, cch=cch, a0=a0, a1=a1, u0=u0, u1=u1: e.scalar_tensor_tensor(
                                    out=cacc[part][:, a0:a1], in0=ub[part][:, u0:u1], scalar=cw(2, cch),
                                    in1=cacc[part][:, a0:a1], op0=ALU.mult, op1=ALU.add),
                                  r=[ubb[part], caccb[part], pvb], w=[caccb[part]])
                        k = self.rot("tmp", NT)
                        A("act", lambda e, k=k, par=par: e.activation(out=tmp[k][:], in_=cacc[par + 1][:], func=AF.Silu),
                          r=[caccb[par + 1]], w=[tmpb[k]])
                        A("dve", lambda e, k=k, t=t, ch=ch, par=par: e.tensor_tensor(out=act[t][:, ch, :], in0=tmp[k][:],
                                                                            in1=cacc[par][:], op=ALU.mult),
                          r=[tmpb[k], caccb[par]], w=[actb[t][ch]])
            for c in range(8):
                slot, sbf = wload([(lambda s: w3(s, 22, 128),
                                    w_down[l].rearrange("(k p) n -> p k n", p=128)[:, :, c * 128:(c + 1) * 128])])
                v = w3(slot, 22, 128)
                for t in self.tiles:
                    ps, psb = self.gp()
                    for kc in range(22):
                        A("pe", lambda e, kc=kc, ps=ps, t=t, v=v: e.matmul(
                            ps[:], lhsT=v[:, kc, :], rhs=act[t][:, kc, :], start=(kc == 0), stop=(kc == 21)),
                          r=[sbf, actb[t][kc]], w=[psb])
                    A("dve", lambda e, ps=ps, t=t, c=c: e.scalar_tensor_tensor(
                        out=xT[t][:, c, :], in0=ps[:], scalar=mod(l, t, 5, c), in1=xT[t][:, c, :],
                        op0=ALU.mult, op1=ALU.add), r=[psb, xb[t][c], modb[l % 2]], w=[xb[t][c]])

        ph = getattr(self, "phase", 9)
        for l in range(self.nlayers):
            adaln(l)
            for t in self.tiles:
                norm_h(l, t, 0)
            for grp in range(5):
                slot, sbf = load_kn(wqkv[l], grp * 512, 512)
                v = w3(slot, 8, 512)
                for t in self.tiles[::-1]:
                    qkv(l, t, grp, v, sbf)
            wk, wv, wsb = mla_load(l)
            mla_kv(l, wk, wv, wsb)
            if LAT:
                gather(l)
            self.gn = 4
            ctx_attn(l)
            pipe_drain()
            if LAT:
                ebu_build(l)
                lat_attn(l)
            pipe_drain()
            self.gn = 7
            merge(l)
            for t in self.tiles[::-1]:
                norm_h(l, t, 1)
            if LAT:
                halo(l)
            ffn(l)
        for t in self.tiles:
            self.outs.append(A("sp", lambda e, t=t: e.dma_start(
                out=y[t].rearrange("(c p) n -> p c n", p=128), in_=xT[t][:]), r=xb[t], dma=True))
        A("sp", lambda e: None, extra_deps=self.outs)
        self.S.emit()
        self.st.close()
        return nc


def _prep_shared(I):
    f = np.float32
    w_in = I["w_in"]
    wq = np.zeros((NL, 1024, NQKV), f)
    qa, ka, va = w_in[:, :, 0:256], w_in[:, :, 256:512], w_in[:, :, 512:768]
    qb, kb, vb = w_in[:, :, 768:1024], w_in[:, :, 1024:1152], w_in[:, :, 1152:1280]
    qc, kc, vc = w_in[:, :, 1280:1536], w_in[:, :, 1536:1664], w_in[:, :, 1664:1792]
    qd, ckv, kpe = w_in[:, :, 1792:2176], w_in[:, :, 2176:2304], w_in[:, :, 2304:2336]

    def gq(q):
        h = q.reshape(NL, 1024, 4, 64)
        return h[:, :, [0, 2, 1, 3], :].reshape(NL, 1024, 256)
    wq[:, :, 0:256] = qa
    wq[:, :, 256:512] = ka
    wq[:, :, 512:768] = gq(qb)
    wq[:, :, 768:896] = kb
    wq[:, :, 896:1024] = kc
    wq[:, :, 1024:1280] = gq(qc)
    wq[:, :, 1280:1408] = ckv
    wq[:, :, 1408:1440] = kpe
    for h in range(4):
        wq[:, :, 1536 + h * 128:1536 + h * 128 + 96] = qd[:, :, h * 96:(h + 1) * 96]
    wq[:, :, 2048:2304] = va
    wq[:, :, 2304:2432] = vb
    wq[:, :, 2432:2560] = vc
    wgate = np.ascontiguousarray(
        w_in[:, :, 2336:].reshape(NL, 1024, 4, 8, 128).transpose(0, 1, 3, 2, 4)).reshape(NL, 1024, 8, 512)
    wb = I["w_branch"]
    perm_nat = np.arange(256)
    perm_gq = np.concatenate([np.arange(64) + 64 * h for h in (0, 2, 1, 3)])
    wbr = np.zeros((NL, 8, 128, 4, 2, 128), f)
    for n in range(4):
        pr = perm_gq if n in (1, 2) else perm_nat
        wn = wb[:, n][:, pr, :]
        wn = wn.reshape(NL, 2, 128, 8, 128)
        wbr[:, :, :, n, :, :] = wn.transpose(0, 3, 2, 1, 4)
    wbr = wbr.reshape(NL, 8, 128, 1024)
    wu = I["w_ukv"].reshape(NL, 128, 4, 128)
    wukv = np.zeros((NL, 128, 640), f)
    for h in range(4):
        wukv[:, :, h * 96:h * 96 + 64] = wu[:, :, h, 0:64]
        wukv[:, :, 384 + h * 64:384 + (h + 1) * 64] = wu[:, :, h, 64:128]
    pv = np.zeros((128, PV_N), f)
    t2 = lambda v: np.concatenate([v, v])
    for l in range(NL):
        o = l * PV_L
        pv[:, o + PV_B:o + PV_B + 48] = I["b_ada"][l].reshape(48, 128).T
        pv[:, o + PV_N1:o + PV_N1 + 8] = I["norm1"][l].reshape(8, 128).T
        pv[:, o + PV_N2:o + PV_N2 + 8] = I["norm2"][l].reshape(8, 128).T
        for j in range(3):
            pv[:, o + PV_CW + j * 44:o + PV_CW + (j + 1) * 44] = I["conv_w"][l, j].reshape(44, 128).T
        pv[:, o + PV_CB:o + PV_CB + 44] = I["conv_b"][l].reshape(44, 128).T
        pv[:, o + PV_QG + 0] = t2(I["qn_a"][l])
        pv[:, o + PV_QG + 1] = t2(I["qn_b"][l])
        pv[:, o + PV_QG + 2] = t2(I["qn_c"][l])
        pv[0:96, o + PV_QG + 3] = I["qn_d"][l]
        pv[:, o + PV_KG + 0] = t2(I["kn_a"][l])
        pv[:, o + PV_KG + 1] = t2(I["kn_b"][l])
        pv[:, o + PV_KG + 2] = t2(I["kn_c"][l])
        pv[0:96, o + PV_KG + 3] = I["kn_d"][l]
        pv[:, o + PV_KG + 4] = I["kvn_d"][l]
        sk = I["sink_c"][l]
        for ch in range(2):
            pv[0:64, o + PV_SK + ch] = sk[ch]
            pv[64:128, o + PV_SK + ch] = sk[ch + 2]
    pv[:, PV_EPS] = EPS
    oh2 = np.zeros((64, 64, 128), f)
    for qc in range(64):
        for kp in range(128):
            dc = int(np.clip((kp % 64) - qc, -15, 15)) + 15
            oh2[(kp // 64) * 32 + dc, qc, kp] = 1.0
    return dict(oh2=oh2.reshape(64, 8192), rpb=np.ascontiguousarray(I["rpb_a"]),
                w_ada=np.ascontiguousarray(I["w_ada"]), wqkv=wq, wgate=wgate, wbr=wbr,
                w_out=np.ascontiguousarray(I["w_out"]), w_up=np.ascontiguousarray(I["w_up"]),
                w_down=np.ascontiguousarray(I["w_down"]), wukv=wukv, pv=pv, cm=_host_consts())


def _core_consts(ql):
    f = np.float32
    s = 512 * ql
    tpos = s + np.arange(512)
    row = (tpos // 64).astype(f)
    col = (tpos % 64).astype(f)

    def cs_tab(dim):
        half = dim // 2
        inv = (10000.0 ** (-np.arange(0, half, 2, dtype=f) / half)).astype(f)
        r = row[:, None] * inv[None, :]
        c = col[:, None] * inv[None, :]
        ang = np.concatenate([r, r, c, c], axis=-1)
        return np.cos(ang).T.astype(f), np.sin(ang).T.astype(f)
    c64, s64 = cs_tab(64)
    c32, s32 = cs_tab(32)
    cs = np.zeros((128, 4, 512), f)
    cs[:, 0, :] = np.concatenate([c64, c64], 0)
    cs[:, 1, :] = np.concatenate([s64, s64], 0)
    cs[0:64, 2, :] = 1.0
    cs[64:96, 2, :] = c32
    cs[64:96, 3, :] = s32
    rw0 = int(np.clip(8 * ql - 4, 0, 16))
    ta0 = 64 * rw0
    tc0 = int(np.clip(512 * ql - 128, 0, 1280))
    maskA = np.zeros((128, 8, 512), f)
    kp = np.arange(128)
    hf, kcol = kp // 64, kp % 64
    qi, qc = np.arange(512) // 64, np.arange(512) % 64
    qr = 8 * ql + qi
    r0 = np.clip(qr - 4, 0, 24)
    c0 = np.clip(qc - 8, 0, 48)
    colok = (kcol[:, None] >= c0[None, :]) & (kcol[:, None] < c0[None, :] + 16)
    for kt in range(8):
        kr = rw0 + 2 * kt + hf
        rowok = (kr[:, None] >= r0[None, :]) & (kr[:, None] < r0[None, :] + 8)
        maskA[:, kt, :] = (rowok & colok).astype(f)
    maskC = np.zeros((128, 6, 512), f)
    for j in range(6):
        ktok = tc0 + 128 * j + kp
        maskC[:, j, :] = (np.abs(ktok[:, None] - tpos[None, :]) <= 128).astype(f)
    rsel = np.zeros((15, 2, NU), f)
    for h2 in range(2):
        for u in range(NU):
            dri = rw0 - 8 * ql + 21 + h2 - u
            if 0 <= dri <= 14:
                rsel[dri, h2, u] = 1.0
    pcf = np.zeros((128, 2), f)
    pcf[:, 0] = 1.0 if ql > 0 else 0.0
    pcf[:, 1] = 1.0 if ql < 3 else 0.0
    meta = np.zeros((1, 8), np.int32)
    meta[0, :6] = [ta0 // 128, tc0 // 128, ta0, tc0, 128 * (2 * max(ql - 1, 0) + 1), 128 * (2 * min(ql + 1, 3))]
    return dict(cs=cs, maskA=maskA, maskC=maskC, rsel=rsel.reshape(15, 2 * NU), pcf=pcf, meta=meta)


_NC_CACHE = {}


def _core_inputs(I, shared, r):
    bl, ql = r // 4, r % 4
    xin = np.zeros((2, 1024, 512), np.float32)
    xin[0] = I["x_prompt"][2 * r:2 * r + 2].reshape(512, 1024).T
    xin[1] = I["x_sample"][bl, 512 * ql:512 * ql + 512].T
    cond = np.zeros((128, 16), np.float32)
    cond[:, 0::2] = I["c_ctx"].reshape(8, 128).T
    cond[:, 1::2] = I["c"][bl].reshape(8, 128).T
    d = dict(shared)
    d["xin"] = xin
    d["cond"] = cond
    d.update(_core_consts(ql))
    tr = lambda a, F: np.ascontiguousarray(a.reshape(NL, 256, F).transpose(0, 2, 1))
    fl = lambda a, F: np.ascontiguousarray(a.reshape(NL, 256, F))
    d["cka"] = tr(I["cache_nat_k"][bl], 256)
    d["cva"] = fl(I["cache_nat_v"][bl], 256)
    d["ckb"] = tr(I["cache_gqa_k"][bl], 128)
    d["cvb"] = fl(I["cache_gqa_v"][bl], 128)
    d["ckc"] = tr(I["cache_win_k"][bl], 128)
    d["cvc"] = fl(I["cache_win_v"][bl], 128)
    d["cckv"] = tr(I["cache_mla_ckv"][bl], 128)
    d["ckpe"] = tr(I["cache_mla_kpe"][bl], 32)
    return d


def kernel(**I):
    import os
    I = {k: np.asarray(v) for k, v in I.items()}
    shared = _prep_shared(I)
    key = "p"
    if key not in _NC_CACHE:
        p = Prog((0, 1))
        p.nlayers = int(os.environ.get("K_NL", NL))
        _NC_CACHE[key] = p.build()
    nc = _NC_CACHE[key]
    in_maps = [_core_inputs(I, shared, r) for r in range(8)]
    res = run_bass_kernel_spmd(nc, in_maps, core_ids=list(range(8)))
    R = res.results
    f = np.float32
    yp = np.zeros((16, 256, 1024), f)
    ys = np.zeros((2, 2048, 1024), f)
    nat_k = np.zeros((16, NL, 256, 4, 64), f)
    nat_v = np.zeros((16, NL, 256, 4, 64), f)
    gqa_k = np.zeros((16, NL, 256, 2, 64), f)
    gqa_v = np.zeros((16, NL, 256, 2, 64), f)
    win_k = np.zeros((16, NL, 256, 2, 64), f)
    win_v = np.zeros((16, NL, 256, 2, 64), f)
    mckv = np.zeros((16, NL, 256, 128), f)
    mkpe = np.zeros((16, NL, 256, 32), f)
    for r in range(8):
        o = R[r]
        bl, ql = r // 4, r % 4
        yp[2 * r:2 * r + 2] = o["y"][0].T.reshape(2, 256, 1024)
        ys[bl, 512 * ql:512 * ql + 512] = o["y"][1].T

        def fm(a, nh):
            L, F, _ = a.shape
            return a.reshape(L, F, 2, 256).transpose(2, 0, 3, 1).reshape(2, L, 256, nh, F // nh)

        def tm(a, nh):
            L, _, F = a.shape
            return a.reshape(L, 2, 256, F).transpose(1, 0, 2, 3).reshape(2, L, 256, nh, F // nh)
        nat_k[2 * r:2 * r + 2] = fm(o["ok_a"], 4)
        nat_v[2 * r:2 * r + 2] = tm(o["ov_a"], 4)
        gqa_k[2 * r:2 * r + 2] = fm(o["ok_b"], 2)
        gqa_v[2 * r:2 * r + 2] = tm(o["ov_b"], 2)
        win_k[2 * r:2 * r + 2] = fm(o["ok_c"], 2)
        win_v[2 * r:2 * r + 2] = tm(o["ov_c"], 2)
        mckv[2 * r:2 * r + 2] = fm(o["o_ckv"], 1).reshape(2, NL, 256, 128)
        mkpe[2 * r:2 * r + 2] = fm(o["o_kpe"], 1).reshape(2, NL, 256, 32)
    return (yp, ys, nat_k, nat_v, gqa_k, gqa_v, win_k, win_v, mckv, mkpe)
```

```python
from contextlib import ExitStack
import numpy as np
import concourse.bass as bass
import concourse.mybir as mybir
from concourse.bass_utils import run_bass_kernel_spmd

F32 = mybir.dt.float32
BF16 = mybir.dt.bfloat16
AF = mybir.ActivationFunctionType
ALU = mybir.AluOpType

ENGS = ("pe", "act", "dve", "pool", "sp")
DMA_RING = 8
DEPTH = 4
NL = DEPTH
EPS = 1e-6


class Buf:
    __slots__ = ("name", "last_w", "readers", "mem")

    def __init__(self, name="", mem=None):
        self.name = name
        self.last_w = None
        self.readers = {}
        self.mem = mem


class Op:
    __slots__ = ("eng", "fn", "deps", "needs_inc", "sem", "semval", "dma", "idx", "inc", "csem")

    def __init__(self, eng, fn, dma, inc=None):
        self.eng = eng
        self.fn = fn
        self.dma = dma
        self.deps = []
        self.needs_inc = False
        self.sem = None
        self.semval = 0
        self.inc = inc
        self.idx = 0
        self.csem = None


class Sched:
    def __init__(self, nc):
        self.nc = nc
        self.ops = {e: [] for e in ENGS}
        self.n_dma = {e: 0 for e in ENGS}
        self._dma_ops = {e: [] for e in ENGS}
        self.live = set()

    def _alias_deps(self, b, deps):
        if b.mem is None or b in self.live:
            return
        lo, hi = b.mem
        for o in list(self.live):
            if o.mem[0] < hi and lo < o.mem[1]:
                if o.last_w is not None:
                    deps[id(o.last_w)] = o.last_w
                for r in o.readers.values():
                    deps[id(r)] = r
                self.live.discard(o)
        self.live.add(b)

    def add(self, eng, fn, reads=(), writes=(), dma=False, extra_deps=(), inc=None, csem=None):
        op = Op(eng, fn, dma, inc)
        op.csem = csem
        deps = {}
        for b in list(reads) + list(writes):
            self._alias_deps(b, deps)
        for b in reads:
            if b.last_w is not None:
                deps[id(b.last_w)] = b.last_w
        for b in writes:
            if b.last_w is not None:
                deps[id(b.last_w)] = b.last_w
            for r in b.readers.values():
                deps[id(r)] = r
        for d in extra_deps:
            deps[id(d)] = d
        for b in reads:
            b.readers[id(op) if dma else eng] = op
        for b in writes:
            b.last_w = op
            b.readers = {}
        op.deps = [d for d in deps.values() if not (d.eng == "pe" and eng == "pe" and not d.dma and not dma)]
        if dma and csem is None:
            op.idx = self.n_dma[eng]
            self.n_dma[eng] += 1
            if op.idx >= DMA_RING:
                op.deps.append(self._dma_ops[eng][op.idx - DMA_RING])
            self._dma_ops[eng].append(op)
        for d in op.deps:
            d.needs_inc = True
        self.ops[eng].append(op)
        return op

    def emit(self):
        nc = self.nc
        with ExitStack() as st:
            esem = {e: st.enter_context(nc.semaphore("s_" + e)) for e in ENGS}
            dsem = {e: [st.enter_context(nc.semaphore("d_%s%d" % (e, i))) for i in range(DMA_RING)]
                    for e in ("sp", "pool")}
            ckeys = sorted({op.csem for e in ENGS for op in self.ops[e] if op.csem is not None})
            csems = {k: st.enter_context(nc.semaphore("c_" + k)) for k in ckeys}
            ccnt = {k: 0 for k in ckeys}
            for e in ENGS:
                cnt = 0
                dcnt = [0] * DMA_RING
                for op in self.ops[e]:
                    if op.csem is not None:
                        ccnt[op.csem] += op.inc
                        op.sem = csems[op.csem]
                        op.semval = ccnt[op.csem]
                        op.needs_inc = True
                    elif op.dma:
                        k = op.idx % DMA_RING
                        dcnt[k] += 16 if op.inc is None else op.inc
                        op.sem = dsem[e][k]
                        op.semval = dcnt[k]
                        op.needs_inc = True
                    elif op.needs_inc:
                        cnt += 1
                        op.sem = esem[e]
                        op.semval = cnt
            block = st.enter_context(nc.Block())

            def make(e):
                def body(eng):
                    waited = {}
                    for op in self.ops[e]:
                        need = {}
                        for d in op.deps:
                            k = id(d.sem)
                            if waited.get(k, 0) >= d.semval:
                                continue
                            if k not in need or need[k][1] < d.semval:
                                need[k] = (d.sem, d.semval)
                        for k, (s, v) in need.items():
                            eng.wait_ge(s, v)
                            waited[k] = v
                        ins = op.fn(eng)
                        if op.needs_inc and ins is not None:
                            ins.then_inc(op.sem, (16 if op.inc is None else op.inc) if op.dma else 1)
                return body

            reg = {"pe": block.tensor, "act": block.scalar, "dve": block.vector,
                   "pool": block.gpsimd, "sp": block.sync}
            for e in ENGS:
                if self.ops[e]:
                    reg[e](make(e))


PV_B = 0
PV_N1 = 48
PV_N2 = 56
PV_CW = 64
PV_CB = 196
PV_QG = 240
PV_KG = 244
PV_SK = 249
PV_L = 251
PV_EPS = NL * PV_L
PV_QS = PV_EPS + 1
PV_ES = PV_QS + NL * 4
PV_N = PV_ES + NL * 2

CM_ONES, CM_BD64, CM_ONES96, CM_SEL, CM_P64, CM_P96 = range(6)
NQKV = 2560


def _host_consts():
    cm = np.zeros((128, 6, 128), np.float32)
    cm[:, CM_ONES, :] = 1.0
    cm[0:64, CM_BD64, 0:64] = 1.0
    cm[64:128, CM_BD64, 64:128] = 1.0
    cm[0:96, CM_ONES96, 0:96] = 1.0
    for i in range(32):
        cm[i, CM_SEL, 64 + i] = 1.0
    def perm(base, sec):
        P = np.zeros((128, 128), np.float32)
        h = sec // 2
        for s0 in base:
            for i in range(h):
                P[s0 + i, s0 + i + h] = -1.0
                P[s0 + i + h, s0 + i] = 1.0
        return P.T.copy()
    cm[:, CM_P64, :] = perm([0, 32, 64, 96], 32)
    cm[:, CM_P96, :] = perm([64, 80], 16)
    return cm.reshape(128, 6 * 128)


NU = 22
KB = 1024


class Prog:
    def __init__(self, tiles=(0, 1)):
        self.tiles = tiles
        self.nc = bass.Bass("TRN2", target_bir_lowering=False)
        self.S = Sched(self.nc)
        self.st = ExitStack()
        self.outs = []
        self.gi = 0
        self.rc = {}
        self.dyn = None

    def A(self, eng, fn, r=(), w=(), **kw):
        return self.S.add(eng, fn, reads=r, writes=w, **kw)

    def din(self, name, shape, dt=F32):
        return self.nc.dram_tensor(name, list(shape), dt, kind="ExternalInput").ap()

    def dout(self, name, shape):
        return self.nc.dram_tensor(name, list(shape), F32, kind="ExternalOutput").ap()

    def sb(self, name, shape, dt):
        return self.st.enter_context(self.nc.sbuf_tensor("sb_" + name, list(shape), dt))

    def ua(self, name, shape, dt, off, nb=1):
        t = self.nc.alloc_sbuf_tensor_at("u_" + name, list(shape), dt, offset=self.ubase + off)
        esz = 4 if dt == F32 else 2
        n = 1
        for d in shape[1:]:
            n *= d
        size = n * esz
        assert off + size <= self.usize, (name, off, size)
        per = size // nb
        bufs = [Buf(name, mem=(off + i * per, off + (i + 1) * per)) for i in range(nb)]
        return t, bufs

    def gp(self):
        n = getattr(self, "gn", 7)
        i = self.gi % n
        self.gi = (i + 1) % n
        return self.pt[i], self.pb[i]

    def rot(self, key, n):
        k = self.rc.get(key, 0)
        self.rc[key] = (k + 1) % n
        return k

    def wslot(self):
        i = self.wi
        self.wi = (self.wi + 1) % len(self.wr)
        return self.wr[i], self.wrb[i]

    def build(self):
        nc = self.nc
        A = self.A
        LAT = 1 in self.tiles
        xin = self.din("xin", [2, 1024, 512])
        cond = self.din("cond", [128, 16])
        pvd = self.din("pv", [128, PV_N])
        cmd = self.din("cm", [128, 768])
        w_ada = self.din("w_ada", [NL, 1024, 6144])
        wqkv = self.din("wqkv", [NL, 1024, NQKV])
        wgate = self.din("wgate", [NL, 1024, 8, 512])
        wbr = self.din("wbr", [NL, 8, 128, 1024])
        w_out = self.din("w_out", [NL, 1024, 1024])
        w_up = self.din("w_up", [NL, 1024, 5632])
        w_down = self.din("w_down", [NL, 8, 128, 2816])
        wukv = self.din("wukv", [NL, 128, 640])
        csd = self.din("cs", [128, 4, 512])
        maskAd = self.din("maskA", [128, 8, 512])
        maskCd = self.din("maskC", [128, 6, 512])
        rseld = self.din("rsel", [15, 2 * NU])
        oh2d = self.din("oh2", [64, 64 * 128])
        rpbd = self.din("rpb", [NL, 4, 15, 31])
        pcfd = self.din("pcf", [128, 2])
        metad = self.nc.dram_tensor("meta", [1, 8], mybir.dt.int32, kind="ExternalInput").ap()
        cka = self.din("cka", [NL, 256, 256])
        cva = self.din("cva", [NL, 256, 256])
        ckb = self.din("ckb", [NL, 128, 256])
        cvb = self.din("cvb", [NL, 256, 128])
        ckc = self.din("ckc", [NL, 128, 256])
        cvc = self.din("cvc", [NL, 256, 128])
        cckv = self.din("cckv", [NL, 128, 256])
        ckpe = self.din("ckpe", [NL, 32, 256])
        y = self.dout("y", [2, 1024, 512])
        ok_a = self.dout("ok_a", [NL, 256, 512])
        ov_a = self.dout("ov_a", [NL, 512, 256])
        ok_b = self.dout("ok_b", [NL, 128, 512])
        ov_b = self.dout("ov_b", [NL, 512, 128])
        ok_c = self.dout("ok_c", [NL, 128, 512])
        ov_c = self.dout("ov_c", [NL, 512, 128])
        o_ckv = self.dout("o_ckv", [NL, 128, 512])
        o_kpe = self.dout("o_kpe", [NL, 32, 512])
        agF_in = nc.dram_tensor("agF_in", [896, 512], BF16).ap()
        agF_out = nc.dram_tensor("agF_out", [4 * 896, 512], BF16).ap()
        KF = nc.dram_tensor("KF", [896, 2048], BF16).ap()
        agT_in = nc.dram_tensor("agT_in", [512, 768], BF16).ap()
        agT_out = nc.dram_tensor("agT_out", [2048, 768], BF16).ap()
        agH_in = nc.dram_tensor("agH_in", [2 * 128, 8], BF16).ap()
        agH_out = nc.dram_tensor("agH_out", [8 * 128, 8], BF16).ap()
        agFin_b, agFout_b, KF_b, agTin_b, agTout_b, agHin_b, agHout_b = [Buf() for _ in range(7)]
        agFin_d, agTin_d = {}, {}
        GROUPS = [[0, 1, 2, 3], [4, 5, 6, 7]]

        sb = self.sb
        xT = [sb("xT%d" % t, [128, 8, 512], F32) for t in range(2)]
        xb = [[Buf() for _ in range(8)] for _ in range(2)]
        hT = [sb("hT%d" % t, [128, 8, 512], BF16) for t in range(2)]
        hb = [[Buf() for _ in range(8)] for _ in range(2)]
        NW = 3
        self.wr = [sb("wr%d" % i, [128, 4096], BF16) for i in range(NW)]
        self.wrb = [Buf() for _ in range(NW)]
        self.wi = 0
        pv = sb("pv", [128, PV_N], F32)
        pvb = Buf()
        cm = sb("cm", [128, 768], BF16)
        cmb = Buf()
        scT = sb("scT", [128, 16], BF16)
        scb = Buf()
        c32 = sb("c32", [128, 16], F32)
        c32b = Buf()
        mods = [sb("mods%d" % i, [128, 96], F32) for i in range(2)]
        modb = [Buf() for _ in range(2)]
        gef = [sb("gef%d" % i, [128, 2, 2, 8], F32) for i in range(2)]
        rinv = sb("rinv", [128, 512], F32)
        rinvb = Buf()
        NT = 4
        tmp = [sb("tmp%d" % i, [128, 512], F32) for i in range(NT)]
        tmpb = [Buf() for _ in range(NT)]
        sq = [sb("sq%d" % i, [128, 512], BF16) for i in range(2)]
        sqb = [Buf() for _ in range(2)]
        ckvn = [sb("ckvn%d" % t, [128, 512], BF16) for t in range(2)]
        ckvb = [Buf(), Buf()]
        kpe = [sb("kpe%d" % t, [32, 512], BF16) for t in range(2)]
        kpeb = [Buf(), Buf()]
        if LAT:
            cs = sb("cs", [128, 4, 512], BF16)
            csb = Buf()
            NKS = 3
            kst = [sb("kst%d" % i, [128, 512], BF16) for i in range(NKS)]
            kstb = [Buf() for _ in range(NKS)]
            rsel = sb("rsel", [15, 2 * NU], BF16)
            rselb = Buf()
            rpbs = sb("rpbs", [15, 4, 32], BF16)
            rpbsb = Buf()
            rpbT2 = sb("rpbT2", [64, 4 * NU], BF16)
            rpbT2b = Buf()
            hh = sb("hh", [128, 2, 8], BF16)
            hhb = Buf()
            pcf = sb("pcf", [128, 2], F32)
            pcfb = Buf()
        USIZE = 102 * KB
        self.usize = USIZE
        ureg = self.st.enter_context(nc.sbuf_tensor("sb_uarena", [128, USIZE], mybir.dt.uint8))
        self.ubase = nc.lookup_mloc(ureg).addr
        ua = self.ua
        Q, Qb, OT, OTb, merged, mgb, act, actb = {}, {}, {}, {}, {}, {}, {}, {}
        o = 0
        for t in range(2):
            for m in "abc":
                Q[t, m], Qb[t, m] = ua("q%s%d" % (m, t), [128, 2, 512], BF16, o, 2)
                o += 2 * KB
            Q[t, "d"], Qb[t, "d"] = ua("qd%d" % t, [128, 4, 512], BF16, o, 4)
            o += 4 * KB
        assert o == 20 * KB
        Kc, Kcb = {}, {}
        Kc["a"], Kcb["a"] = ua("ka", [128, 2, 512], BF16, 20 * KB, 2)
        Kc["b"], Kcb["b"] = ua("kb", [128, 1, 512], BF16, 22 * KB, 1)
        Kc["c"], Kcb["c"] = ua("kc", [128, 1, 512], BF16, 23 * KB, 1)
        Kc["d"], Kcb["d"] = ua("kd", [128, 4, 512], BF16, 24 * KB, 4)
        Vt, vtb_ = ua("Vt", [128, 4, 768], BF16, 28 * KB, 1)
        Vtb = [[Buf(mem=(28 * KB + tb * 1536, 28 * KB + tb * 1536 + 1024)),
                Buf(mem=(28 * KB + tb * 1536 + 1024, 28 * KB + (tb + 1) * 1536))] for tb in range(4)]
        NPT = 5
        PT, PTb = [], []
        for i in range(NPT):
            t_, b_ = ua("PT%d" % i, [128, 512], BF16, 48 * KB + i * KB, 1)
            PT.append(t_)
            PTb.append(b_[0])
        for t in range(2):
            for mi, m in enumerate("abcd"):
                OT[t, m], OTb[t, m] = ua("o%s%d" % (m, t), [128, 2, 512], BF16, 53 * KB + (t * 4 + mi) * 2 * KB, 2)
        gate, gateb = [], []
        for i in range(2):
            t_, b_ = ua("gate%d" % i, [128, 512], BF16, 69 * KB + i * KB, 1)
            gate.append(t_)
            gateb.append(b_[0])
        macc, mb_ = ua("macc", [128, 512], F32, 71 * KB, 1)
        maccb = mb_[0]
        for t in range(2):
            merged[t], mgb[t] = ua("mg%d" % t, [128, 8, 512], BF16, t * 8 * KB, 8)
        if LAT:
            KA, KAb = ua("KA", [128, 2, 1280], BF16, 0, 1)
            VA, VAb = ua("VA", [128, 10, 256], BF16, 5 * KB, 1)
            KD, KDb = ua("KD", [128, 4, 2304], BF16, 20 * KB, 1)
            VD, VDb = ua("VD", [128, 18, 256], BF16, 20 * KB + 18432, 1)
            OH2, OH2b = ua("OH2", [64, 64, 128], BF16, 20 * KB, 1)
            cckvs, cckvsb = ua("cckvs", [128, 256], BF16, 47 * KB, 1)
            ckpes, ckpesb = ua("ckpes", [32, 256], BF16, 47 * KB + 512, 1)
            EBu, EBub = ua("EBu", [128, 4 * NU, 64], BF16, 73 * KB, 1)
            KBa, KBab = ua("KBa", [128, 2304], BF16, 73 * KB, 1)
            VBa, VBab = ua("VBa", [128, 18, 128], BF16, 73 * KB + 4608, 1)
            KCa, KCab = ua("KCa", [128, 1024], BF16, 73 * KB + 9216, 1)
            VCa, VCab = ua("VCa", [128, 8, 128], BF16, 73 * KB + 9216 + 2048, 1)
            assert 73 * KB + 9216 + 4096 <= 88 * KB
            maskA, maskAb = ua("maskA", [128, 8, 512], BF16, 88 * KB, 1)
            maskC, maskCb = ua("maskC", [128, 6, 512], BF16, 96 * KB, 1)
        for t in range(2):
            act[t], actb[t] = ua("act%d" % t, [128, 22, 512], BF16, t * 22 * KB, 22)
        ub, ubb, cacc, caccb = [], [], [], []
        for i in range(4):
            t_, b_ = ua("ub%d" % i, [128, 514], F32, 44 * KB + i * 2080, 1)
            ub.append(t_)
            ubb.append(b_[0])
            t_, b_ = ua("cacc%d" % i, [128, 512], F32, 44 * KB + 8320 + i * 2048, 1)
            cacc.append(t_)
            caccb.append(b_[0])
        self.pt = [self.st.enter_context(nc.psum_tensor("ps%d" % i, [128, 512], F32)) for i in range(8)]
        self.pb = [Buf() for _ in range(8)]
        pt, pb = self.pt, self.pb

        def cmat(i, k=128, m=128):
            return cm[0:k, i * 128:i * 128 + m]

        def sp_init(e):
            if LAT:
                regs = [e.alloc_register("mr%d" % i) for i in range(6)]
                for i, rg in enumerate(regs):
                    e.reg_load(rg, metad[0:1, i:i + 1])
                mx = [8, 10, 1024, 1280, 896, 896]
                self.dyn = [e.snap(rg, min_val=0, max_val=mx[i]) for i, rg in enumerate(regs)]
            return e.dma_start(out=pv[:], in_=pvd)
        A("sp", sp_init, w=[pvb], dma=True)
        A("pool", lambda e: e.dma_start(out=cm[:], in_=cmd), w=[cmb], dma=True)
        A("sp", lambda e: e.dma_start(out=c32[:], in_=cond), w=[c32b], dma=True)
        for t in self.tiles:
            A("sp", lambda e, t=t: e.dma_start(out=xT[t][:], in_=xin[t].rearrange("(c p) n -> p c n", p=128)),
              w=xb[t], dma=True)
        if LAT:
            A("pool", lambda e: e.dma_start(out=cs[:], in_=csd), w=[csb], dma=True)
            A("sp", lambda e: e.dma_start(out=pcf[:], in_=pcfd), w=[pcfb], dma=True)
            A("pool", lambda e: e.dma_start(out=rsel[:], in_=rseld), w=[rselb], dma=True)
            A("pool", lambda e: e.dma_start(out=maskA[:], in_=maskAd), w=maskAb, dma=True)
            A("pool", lambda e: e.dma_start(out=maskC[:], in_=maskCd), w=maskCb, dma=True)
            A("dve", lambda e: e.memset(rpbs[:], 0.0), w=[rpbsb])
        A("act", lambda e: e.activation(out=scT[:], in_=c32[:], func=AF.Silu), r=[c32b], w=[scb])
        for l in range(NL):
            o = l * PV_L
            A("dve", lambda e, l=l, o=o: e.tensor_scalar(out=pv[:, PV_QS + 4 * l:PV_QS + 4 * l + 3],
                                                          in0=pv[:, o + PV_QG:o + PV_QG + 3], scalar1=0.125,
                                                          scalar2=None, op0=ALU.mult), r=[pvb], w=[pvb])
            A("dve", lambda e, l=l, o=o: e.tensor_scalar(out=pv[:, PV_QS + 4 * l + 3:PV_QS + 4 * l + 4],
                                                          in0=pv[:, o + PV_QG + 3:o + PV_QG + 4],
                                                          scalar1=float(96 ** -0.5), scalar2=None, op0=ALU.mult),
              r=[pvb], w=[pvb])
            A("act", lambda e, l=l, o=o: e.activation(out=pv[:, PV_ES + 2 * l:PV_ES + 2 * l + 2],
                                                       in_=pv[:, o + PV_SK:o + PV_SK + 2], func=AF.Exp),
              r=[pvb], w=[pvb])
        eps = pv[:, PV_EPS:PV_EPS + 1]

        def wload(view_src_pairs):
            slot, sbuf_ = self.wslot()
            for k, (dst_fn, src) in enumerate(view_src_pairs):
                A("pool", lambda e, dst_fn=dst_fn, src=src, slot=slot: e.dma_start(out=dst_fn(slot), in_=src),
                  w=[sbuf_], dma=True)
            return slot, sbuf_

        def w3(slot, k, n):
            return slot[:, 0:k * n].rearrange("p (k n) -> p k n", k=k)

        def load_kn(src2d, c0, ncols, k=8):
            return wload([(lambda s: w3(s, k, ncols),
                           src2d.rearrange("(k p) n -> p k n", p=128)[:, :, c0:c0 + ncols])])

        def adaln(l):
            mb = l % 2
            pm, pmb = pt[7], pb[7]
            for g in range(12):
                slot, sbf = load_kn(w_ada[l], g * 512, 512)
                v = w3(slot, 8, 512)
                for j in range(4):
                    col = (g * 4 + j) * 2
                    for kc in range(8):
                        A("pe", lambda e, v=v, j=j, kc=kc, col=col: e.matmul(
                            pm[:, col:col + 2], lhsT=v[:, kc, j * 128:(j + 1) * 128],
                            rhs=scT[:, kc * 2:kc * 2 + 2], start=(kc == 0), stop=(kc == 7)),
                          r=[sbf, scb], w=[pmb])
            m3 = mods[mb][:, 0:96].rearrange("p (a b) -> p a b", b=2)
            p3 = pm[:, 0:96].rearrange("p (a b) -> p a b", b=2)
            o = l * PV_L
            for t in range(2):
                A("dve", lambda e, t=t: e.tensor_tensor(out=m3[:, :, t], in0=p3[:, :, t],
                                                        in1=pv[:, o + PV_B:o + PV_B + 48], op=ALU.add),
                  r=[pmb, pvb], w=[modb[mb]])
            for t in range(2):
                for ni, (sc0, nv) in enumerate(((8, PV_N1), (32, PV_N2))):
                    A("dve", lambda e, t=t, ni=ni, sc0=sc0, nv=nv: e.scalar_tensor_tensor(
                        out=gef[mb][:, ni, t, :], in0=m3[:, sc0:sc0 + 8, t], scalar=1.0,
                        in1=pv[:, o + nv:o + nv + 8], op0=ALU.add, op1=ALU.mult),
                      r=[modb[mb], pvb], w=[modb[mb]])

        def mod(l, t, idx, c):
            col = ((idx * 8 + c) * 2 + t)
            return mods[l % 2][:, col:col + 1]

        def norm_h(l, t, ni):
            ps, psb = self.gp()
            for c in range(8):
                k = self.rot("sq", 2)
                A("act", lambda e, c=c, k=k: e.activation(out=sq[k][:], in_=xT[t][:, c, :], func=AF.Square),
                  r=[xb[t][c]], w=[sqb[k]])
                A("pe", lambda e, c=c, k=k: e.matmul(ps[:], lhsT=cmat(CM_ONES), rhs=sq[k][:],
                                                     start=(c == 0), stop=(c == 7)),
                  r=[sqb[k], cmb], w=[psb])
            A("act", lambda e: e.activation(out=rinv[:], in_=ps[:], func=AF.Ln, scale=1.0 / 1024, bias=eps),
              r=[psb, pvb], w=[rinvb])
            A("act", lambda e: e.activation(out=rinv[:], in_=rinv[:], func=AF.Exp, scale=-0.5), r=[rinvb], w=[rinvb])
            for c in range(8):
                k = self.rot("tmp", NT)
                A("dve", lambda e, c=c, k=k: e.tensor_tensor(out=tmp[k][:], in0=xT[t][:, c, :], in1=rinv[:],
                                                             op=ALU.mult),
                  r=[xb[t][c], rinvb], w=[tmpb[k]])
                A("act", lambda e, c=c, k=k: e.activation(
                    out=hT[t][:, c, :], in_=tmp[k][:], func=AF.Identity,
                    scale=gef[l % 2][:, ni, t, c:c + 1], bias=mod(l, t, 3 * ni, c)),
                  r=[tmpb[k], modb[l % 2]], w=[hb[t][c]])

        def proj_fm(v, sbf, j0, M, t):
            ps, psb = self.gp()
            for kc in range(8):
                A("pe", lambda e, kc=kc: e.matmul(ps[0:M, :], lhsT=v[:, kc, j0:j0 + M], rhs=hT[t][:, kc, :],
                                                  start=(kc == 0), stop=(kc == 7)),
                  r=[sbf, hb[t][kc]], w=[psb])
            return ps, psb

        def headnorm(ps, psb, M, cmi, invn, gain, dst, dstb, want32=False, N=512):
            k = self.rot("sq", 2)
            A("act", lambda e: e.activation(out=sq[k][0:M, 0:N], in_=ps[0:M, 0:N], func=AF.Square),
              r=[psb], w=[sqb[k]])
            p2, p2b = self.gp()
            A("pe", lambda e: e.matmul(p2[0:M, 0:N], lhsT=cmat(cmi, M, M), rhs=sq[k][0:M, 0:N], start=True, stop=True),
              r=[sqb[k], cmb], w=[p2b])
            k1 = self.rot("tmp", NT)
            A("act", lambda e: e.activation(out=tmp[k1][0:M, 0:N], in_=p2[0:M, 0:N], func=AF.Ln, scale=invn,
                                            bias=eps[0:M]), r=[p2b, pvb], w=[tmpb[k1]])
            A("act", lambda e: e.activation(out=tmp[k1][0:M, 0:N], in_=tmp[k1][0:M, 0:N], func=AF.Exp, scale=-0.5),
              r=[tmpb[k1]], w=[tmpb[k1]])
            if want32:
                k2 = self.rot("tmp", NT)
                A("dve", lambda e: e.scalar_tensor_tensor(out=tmp[k2][0:M, 0:N], in0=ps[0:M, 0:N], scalar=gain[0:M],
                                                          in1=tmp[k1][0:M, 0:N], op0=ALU.mult, op1=ALU.mult),
                  r=[psb, tmpb[k1], pvb], w=[tmpb[k2]])
                A("act", lambda e: e.activation(out=dst, in_=tmp[k2][0:M, 0:N], func=AF.Identity), r=[tmpb[k2]], w=[dstb])
                return k2
            A("dve", lambda e: e.scalar_tensor_tensor(out=dst, in0=ps[0:M, 0:N], scalar=gain[0:M],
                                                      in1=tmp[k1][0:M, 0:N], op0=ALU.mult, op1=ALU.mult),
              r=[psb, tmpb[k1], pvb], w=[dstb])
            return None

        def store(dst, k, M):
            self.outs.append(A("sp", lambda e: e.dma_start(out=dst, in_=tmp[k][0:M, :]), r=[tmpb[k]], dma=True))

        def rope(src, srcb, M, pidx, ci, dst, dstb):
            p3_, p3b = self.gp()
            A("pe", lambda e: e.matmul(p3_[0:M, :], lhsT=cmat(pidx, M, M), rhs=src, start=True, stop=True),
              r=[srcb, cmb], w=[p3b])
            k1 = self.rot("tmp", NT)
            A("dve", lambda e: e.tensor_tensor(out=tmp[k1][0:M, :], in0=src, in1=cs[0:M, ci, :], op=ALU.mult),
              r=[srcb, csb], w=[tmpb[k1]])
            k2 = self.rot("tmp", NT)
            A("dve", lambda e: e.tensor_tensor(out=tmp[k2][0:M, :], in0=p3_[0:M, :], in1=cs[0:M, ci + 1, :],
                                               op=ALU.mult), r=[p3b, csb], w=[tmpb[k2]])
            A("dve", lambda e: e.tensor_tensor(out=dst, in0=tmp[k1][0:M, :], in1=tmp[k2][0:M, :], op=ALU.add),
              r=[tmpb[k1], tmpb[k2]], w=[dstb])

        def lat_k(ps, psb, M, cmi, invn, gain, do_rope, pidx, ci, row0):
            k = self.rot("kst", NKS)
            headnorm(ps, psb, M, cmi, invn, gain, kst[k][0:M, :], kstb[k])
            if do_rope:
                k2 = self.rot("kst", NKS)
                rope(kst[k][0:M, :], kstb[k], M, pidx, ci, kst[k2][0:M, :], kstb[k2])
                k = k2
            A("sp", lambda e: e.dma_start(out=agF_in[row0:row0 + M, :], in_=kst[k][0:M, :]),
              r=[kstb[k]], w=[agFin_d.setdefault(row0, Buf())], dma=True)

        def qkv(l, t, grp, v, sbf):
            o = l * PV_L
            qs = lambda i: pv[:, PV_QS + 4 * l + i:PV_QS + 4 * l + i + 1]
            kg = lambda i: pv[:, o + PV_KG + i:o + PV_KG + i + 1]
            lat = (t == 1)
            if grp == 0:
                for j in range(2):
                    ps, psb = proj_fm(v, sbf, j * 128, 128, t)
                    headnorm(ps, psb, 128, CM_BD64, 1 / 64., qs(0), Q[t, "a"][:, j, :], [Qb[t, "a"][j]][0])
                for j in range(2):
                    ps, psb = proj_fm(v, sbf, 256 + j * 128, 128, t)
                    if lat:
                        lat_k(ps, psb, 128, CM_BD64, 1 / 64., kg(0), False, 0, 0, j * 128)
                    else:
                        k2 = headnorm(ps, psb, 128, CM_BD64, 1 / 64., kg(0), Kc["a"][:, j, :], Kcb["a"][j], want32=True)
                        store(ok_a[l, j * 128:(j + 1) * 128, :], k2, 128)
            elif grp == 1:
                for j in range(2):
                    ps, psb = proj_fm(v, sbf, j * 128, 128, t)
                    headnorm(ps, psb, 128, CM_BD64, 1 / 64., qs(1), Q[t, "b"][:, j, :], Qb[t, "b"][j])
                    if lat:
                        rope(Q[t, "b"][:, j, :], Qb[t, "b"][j], 128, CM_P64, 0, Q[t, "b"][:, j, :], Qb[t, "b"][j])
                for j, (m, okd) in enumerate((("b", ok_b), ("c", ok_c))):
                    ps, psb = proj_fm(v, sbf, 256 + j * 128, 128, t)
                    if lat:
                        lat_k(ps, psb, 128, CM_BD64, 1 / 64., kg(1 + j), True, CM_P64, 0, 256 + j * 128)
                    else:
                        k2 = headnorm(ps, psb, 128, CM_BD64, 1 / 64., kg(1 + j), Kc[m][:, 0, :], Kcb[m][0], want32=True)
                        store(okd[l], k2, 128)
            elif grp == 2:
                for j in range(2):
                    ps, psb = proj_fm(v, sbf, j * 128, 128, t)
                    headnorm(ps, psb, 128, CM_BD64, 1 / 64., qs(2), Q[t, "c"][:, j, :], Qb[t, "c"][j])
                    if lat:
                        rope(Q[t, "c"][:, j, :], Qb[t, "c"][j], 128, CM_P64, 0, Q[t, "c"][:, j, :], Qb[t, "c"][j])
                ps, psb = proj_fm(v, sbf, 256, 128, t)
                k2 = headnorm(ps, psb, 128, CM_ONES, 1 / 128., kg(4), ckvn[t][:], ckvb[t], want32=True)
                if not lat:
                    store(o_ckv[l], k2, 128)
                ps, psb = proj_fm(v, sbf, 384, 32, t)
                k3 = self.rot("tmp", NT)
                A("act", lambda e: e.activation(out=tmp[k3][0:32, :], in_=ps[0:32, :], func=AF.Identity),
                  r=[psb], w=[tmpb[k3]])
                A("act", lambda e: e.activation(out=kpe[t][:], in_=tmp[k3][0:32, :], func=AF.Identity), r=[tmpb[k3]], w=[kpeb[t]])
                if not lat:
                    store(o_kpe[l], k3, 32)
            elif grp == 3:
                for h in range(4):
                    ps, psb = proj_fm(v, sbf, h * 128, 96, t)
                    headnorm(ps, psb, 96, CM_ONES96, 1 / 96., qs(3), Q[t, "d"][0:96, h, :], Qb[t, "d"][h])
                    if lat:
                        rope(Q[t, "d"][0:96, h, :], Qb[t, "d"][h], 96, CM_P96, 2, Q[t, "d"][0:96, h, :], Qb[t, "d"][h])
            elif grp == 4:
                for tb in range(4):
                    ps, psb = self.gp()
                    for kc in range(8):
                        A("pe", lambda e, kc=kc, ps=ps, tb=tb: e.matmul(
                            ps[:], lhsT=hT[t][:, kc, tb * 128:(tb + 1) * 128], rhs=v[:, kc, :],
                            start=(kc == 0), stop=(kc == 7)), r=[sbf, hb[t][kc]], w=[psb])
                    if lat:
                        k = self.rot("kst", NKS)
                        A("act", lambda e, k=k, ps=ps: e.activation(out=kst[k][:], in_=ps[:], func=AF.Identity),
                          r=[psb], w=[kstb[k]])
                        A("sp", lambda e, k=k, tb=tb: e.dma_start(out=agT_in[tb * 128:(tb + 1) * 128, 0:512],
                                                                   in_=kst[k][:]), r=[kstb[k]], w=[agTin_d.setdefault((tb, 0), Buf())], dma=True)
                        continue
                    k = self.rot("tmp", NT)
                    A("act", lambda e, k=k, ps=ps: e.activation(out=tmp[k][:], in_=ps[:], func=AF.Identity),
                      r=[psb], w=[tmpb[k]])
                    A("act", lambda e, k=k, tb=tb: e.activation(out=Vt[:, tb, 0:512], in_=tmp[k][:], func=AF.Identity),
                      r=[tmpb[k]], w=[Vtb[tb][0]])
                    for dst, c0, n in ((ov_a, 0, 256), (ov_b, 256, 128), (ov_c, 384, 128)):
                        self.outs.append(A("sp", lambda e, k=k, dst=dst, c0=c0, n=n, tb=tb: e.dma_start(
                            out=dst[l, tb * 128:(tb + 1) * 128, :], in_=tmp[k][:, c0:c0 + n]), r=[tmpb[k]], dma=True))

        def mla_load(l):
            slot, sbf = wload([(lambda s: s[:, 0:640], wukv[l])])
            wk = slot[:, 0:384].rearrange("p (h n) -> p h n", h=4)
            wv = slot[:, 384:640]
            return wk, wv, sbf

        def mla_kraw(wk, sbf, h, ck, ckb_, kp, kpb_, N):
            ps, psb = self.gp()
            A("pe", lambda e: e.matmul(ps[0:96, 0:N], lhsT=wk[:, h, :], rhs=ck, start=True, stop=False),
              r=[sbf, ckb_], w=[psb])
            A("pe", lambda e: e.matmul(ps[0:96, 0:N], lhsT=cm[0:32, CM_SEL * 128:CM_SEL * 128 + 96], rhs=kp,
                                       start=False, stop=True), r=[cmb, kpb_], w=[psb])
            return ps, psb

        def mla_kv(l, wk, wv, sbf):
            o = l * PV_L
            kdg = pv[:, o + PV_KG + 3:o + PV_KG + 4]
            for t in self.tiles:
                for h in range(4):
                    ps, psb = mla_kraw(wk, sbf, h, ckvn[t][:], ckvb[t], kpe[t][:], kpeb[t], 512)
                    if t == 0:
                        headnorm(ps, psb, 96, CM_ONES96, 1 / 96., kdg, Kc["d"][0:96, h, :], Kcb["d"][h])
                    else:
                        lat_k(ps, psb, 96, CM_ONES96, 1 / 96., kdg, True, CM_P96, 2, 512 + 96 * h)
                for tb in range(4):
                    ps, psb = self.gp()
                    A("pe", lambda e, tb=tb, ps=ps, t=t: e.matmul(ps[:, 0:256], lhsT=ckvn[t][:, tb * 128:(tb + 1) * 128],
                                                                 rhs=wv, start=True, stop=True),
                      r=[sbf, ckvb[t]], w=[psb])
                    if t == 0:
                        A("act", lambda e, tb=tb, ps=ps: e.activation(out=Vt[:, tb, 512:768], in_=ps[:, 0:256],
                                                                      func=AF.Identity), r=[psb], w=[Vtb[tb][1]])
                    else:
                        k = self.rot("kst", NKS)
                        A("act", lambda e, k=k, ps=ps: e.activation(out=kst[k][:, 0:256], in_=ps[:, 0:256],
                                                                    func=AF.Identity), r=[psb], w=[kstb[k]])
                        A("sp", lambda e, k=k, tb=tb: e.dma_start(out=agT_in[tb * 128:(tb + 1) * 128, 512:768],
                                                                   in_=kst[k][:, 0:256]), r=[kstb[k]], w=[agTin_d.setdefault((tb, 1), Buf())],
                          dma=True)

        def attn_chunk(t, m, ch, heads, l, sink=False):
            ai = self.rot("acc", 2)
            po, pob = pt[4 + ai], pb[4 + ai]
            pl, plb = pt[6 + ai], pb[6 + ai]
            seq = []
            for half, hd in enumerate(heads):
                first = {}
                for tl in hd["tiles"]:
                    grp = tl[7]
                    st_ = grp not in first
                    first[grp] = True
                    seq.append((half, hd, tl, st_, hd["last"][grp] is tl[0]))
            DEP = 3
            pend = {}

            def stage1(i):
                half, hd, (kv, kbuf, vv, vbuf, q0, q1, post, grp), st_, last = seq[i]
                qv, qbuf = hd["q"]
                n = q1 - q0
                ps, psb = self.gp()
                A("pe", lambda e: e.matmul(ps[:, 0:n], lhsT=kv, rhs=qv[:, q0:q1], start=True, stop=True),
                  r=[kbuf, qbuf], w=[psb])
                k = self.rot("PT", NPT)
                A("act", lambda e: e.activation(out=PT[k][:, 0:n], in_=ps[:, 0:n], func=AF.Exp), r=[psb], w=[PTb[k]])
                if post is not None:
                    for (mk, mkb) in post:
                        A("dve", lambda e, mk=mk: e.tensor_tensor(out=PT[k][:, 0:n], in0=PT[k][:, 0:n], in1=mk,
                                                                  op=ALU.mult), r=[PTb[k]] + mkb, w=[PTb[k]])
                pend[i] = k

            def stage2(i):
                half, hd, (kv, kbuf, vv, vbuf, q0, q1, post, grp), st_, last = seq[i]
                n = q1 - q0
                k = pend.pop(i)
                osl = slice(half * 64, half * 64 + 64)
                A("pe", lambda e: e.matmul(po[osl, q0:q1], lhsT=vv, rhs=PT[k][:, 0:n], start=st_, stop=last),
                  r=[PTb[k], vbuf], w=[pob])
                A("pe", lambda e: e.matmul(pl[osl, q0:q1], lhsT=cm[:, 0:64], rhs=PT[k][:, 0:n], start=st_, stop=last),
                  r=[PTb[k], cmb], w=[plb])
            for i in range(len(seq) + DEP):
                if i < len(seq):
                    stage1(i)
                if i - DEP >= 0:
                    stage2(i - DEP)
            k1 = self.rot("tmp", NT)
            if sink:
                A("act", lambda e: e.activation(out=tmp[k1][:], in_=pl[:], func=AF.Ln,
                                                bias=pv[:, PV_ES + 2 * l + ch:PV_ES + 2 * l + ch + 1]),
                  r=[plb, pvb], w=[tmpb[k1]])
            else:
                A("act", lambda e: e.activation(out=tmp[k1][:], in_=pl[:], func=AF.Ln), r=[plb], w=[tmpb[k1]])
            A("act", lambda e: e.activation(out=tmp[k1][:], in_=tmp[k1][:], func=AF.Exp, scale=-1.0),
              r=[tmpb[k1]], w=[tmpb[k1]])
            A("dve", lambda e: e.tensor_tensor(out=OT[t, m][:, ch, :], in0=po[:], in1=tmp[k1][:], op=ALU.mult),
              r=[pob, tmpb[k1]], w=[OTb[t, m][ch]])

        def ctx_attn(l):
            t = 0
            for m in "abcd":
                for ch in range(2):
                    heads = []
                    for half in range(2):
                        if m == "d":
                            hd = 2 * ch + half
                            q = (Q[t, "d"][0:96, hd, :], Qb[t, "d"][hd])
                            kfull, kbuf = Kc["d"][0:96, hd, :], Kcb["d"][hd]
                            vcol, vsel = 512 + hd * 64, 1
                        else:
                            q = (Q[t, m][half * 64:half * 64 + 64, ch, :], Qb[t, m][ch])
                            if m == "a":
                                kfull, kbuf = Kc["a"][half * 64:half * 64 + 64, ch, :], Kcb["a"][ch]
                                vcol = (2 * ch + half) * 64
                            else:
                                kfull, kbuf = Kc[m][half * 64:half * 64 + 64, 0, :], Kcb[m][0]
                                vcol = (256 if m == "b" else 384) + half * 64
                            vsel = 0
                        tiles = []
                        last = {}
                        for b in range(2):
                            for kt in range(2):
                                tb = 2 * b + kt
                                kv = kfull[:, tb * 128:(tb + 1) * 128]
                                tiles.append((kv, kbuf, Vt[:, tb, vcol:vcol + 64], Vtb[tb][vsel],
                                              b * 256, (b + 1) * 256, None, b))
                                last[b] = kv
                        heads.append(dict(q=q, tiles=tiles, last=last))
                    attn_chunk(0, m, ch, heads, l, sink=(m == "c"))

        def gather(l):
            A("pool", lambda e: e.collective_compute("AllGather", ALU.bypass, replica_groups=GROUPS,
                                                     ins=[agF_in], outs=[agF_out]),
              r=list(agFin_d.values()), w=[agFout_b], dma=True, inc=1, csem="F")
            A("pool", lambda e: e.collective_compute("AllGather", ALU.bypass, replica_groups=GROUPS,
                                                     ins=[agT_in], outs=[agT_out]),
              r=list(agTin_d.values()), w=[agTout_b], dma=True, inc=1, csem="T")
            for q in range(4):
                A("sp", lambda e, q=q: e.dma_start(out=KF[:, 512 * q:512 * (q + 1)],
                                                   in_=agF_out[896 * q:896 * (q + 1), :]),
                  r=[agFout_b], w=[KF_b], dma=True)

        agT_v = agT_out.rearrange("(t p) c -> p t c", p=128)

        def ebu_build(l):
            A("pool", lambda e: e.dma_start(out=rpbs[:, :, 0:31], in_=rpbd[l].rearrange("h r c -> r h c")),
              w=[rpbsb], dma=True)
            A("pool", lambda e: e.dma_start(out=OH2[:].rearrange("p a b -> p (a b)"), in_=oh2d), w=OH2b, dma=True)
            pr, prb = self.gp()
            for h in range(4):
                for hf in range(2):
                    A("pe", lambda e, h=h, hf=hf: e.matmul(
                        pr[hf * 32:hf * 32 + 32, h * NU:(h + 1) * NU], lhsT=rpbs[:, h, :],
                        rhs=rsel[:, hf * NU:(hf + 1) * NU], start=True, stop=True),
                      r=[rpbsb, rselb], w=[prb])
            A("act", lambda e: e.activation(out=rpbT2[:], in_=pr[0:64, 0:4 * NU], func=AF.Identity),
              r=[prb], w=[rpbT2b])
            for g in range(16):
                pe_, peb = self.gp()
                for qq in range(4):
                    qc = g * 4 + qq
                    A("pe", lambda e, qc=qc, qq=qq, pe_=pe_: e.matmul(
                        pe_[:, qq * 4 * NU:(qq + 1) * 4 * NU], lhsT=OH2[:, qc, :], rhs=rpbT2[:], start=True, stop=True),
                      r=OH2b + [rpbT2b], w=[peb])
                A("act", lambda e, g=g, pe_=pe_: e.activation(
                    out=EBu[:, :, g * 4:(g + 1) * 4],
                    in_=pe_[:, 0:16 * NU].rearrange("p (q n) -> p n q", q=4), func=AF.Exp),
                  r=[peb], w=EBub)

        def lat_attn(l):
            t = 1
            d = lambda i: self.dyn[i]
            A("pool", lambda e: e.dma_start(out=KA[:, :, 0:256], in_=cka[l].rearrange("(c p) n -> p c n", p=128)),
              w=KAb, dma=True)
            A("pool", lambda e: e.dma_start(out=VA[:, 0:2, :], in_=cva[l].rearrange("(t p) c -> p t c", p=128)),
              w=VAb, dma=True)
            for c in range(2):
                A("sp", lambda e, c=c: e.dma_start(out=KA[:, c, 256:1280],
                                                   in_=KF[c * 128:(c + 1) * 128, bass.ds(d(2), 1024)]),
                  r=[KF_b], w=KAb, dma=True)
            A("sp", lambda e: e.dma_start(out=VA[:, 2:10, :], in_=agT_v[:, bass.ds(d(0), 8), 0:256]),
              r=[agTout_b], w=VAb, dma=True)
            for ch in range(2):
                heads = []
                for half in range(2):
                    h = 2 * ch + half
                    q = (Q[t, "a"][half * 64:half * 64 + 64, ch, :], Qb[t, "a"][ch])
                    tiles = []
                    for kt in range(10):
                        kv = KA[half * 64:half * 64 + 64, ch, kt * 128:(kt + 1) * 128]
                        post = None
                        if kt >= 2:
                            u0 = 14 - 2 * (kt - 2)
                            post = [(EBu[:, h * NU + u0:h * NU + u0 + 8, :].rearrange("p a b -> p (a b)"), EBub),
                                    (maskA[:, kt - 2, :], maskAb)]
                        tiles.append((kv, KAb[0], VA[:, kt, h * 64:(h + 1) * 64], VAb[0], 0, 512, post, 0))
                    heads.append(dict(q=q, tiles=tiles, last={0: tiles[-1][0]}))
                attn_chunk(1, "a", ch, heads, l)
            A("pool", lambda e: e.dma_start(out=KCa[:, 0:256], in_=ckc[l]), w=KCab, dma=True)
            A("pool", lambda e: e.dma_start(out=VCa[:, 0:2, :], in_=cvc[l].rearrange("(t p) c -> p t c", p=128)),
              w=VCab, dma=True)
            A("sp", lambda e: e.dma_start(out=KCa[:, 256:1024], in_=KF[384:512, bass.ds(d(3), 768)]),
              r=[KF_b], w=KCab, dma=True)
            A("sp", lambda e: e.dma_start(out=VCa[:, 2:8, :], in_=agT_v[:, bass.ds(d(1), 6), 384:512]),
              r=[agTout_b], w=VCab, dma=True)
            for ch in range(2):
                heads = []
                for half in range(2):
                    q = (Q[t, "c"][half * 64:half * 64 + 64, ch, :], Qb[t, "c"][ch])
                    tiles = []
                    for kt in range(8):
                        kv = KCa[half * 64:half * 64 + 64, kt * 128:(kt + 1) * 128]
                        post = [(maskC[:, kt - 2, :], maskCb)] if kt >= 2 else None
                        tiles.append((kv, KCab[0], VCa[:, kt, half * 64:(half + 1) * 64], VCab[0], 0, 512, post, 0))
                    heads.append(dict(q=q, tiles=tiles, last={0: tiles[-1][0]}))
                attn_chunk(1, "c", ch, heads, l, sink=True)
            A("pool", lambda e: e.dma_start(out=KBa[:, 0:256], in_=ckb[l]), w=KBab, dma=True)
            A("pool", lambda e: e.dma_start(out=VBa[:, 0:2, :], in_=cvb[l].rearrange("(t p) c -> p t c", p=128)),
              w=VBab, dma=True)
            A("sp", lambda e: e.dma_start(out=KBa[:, 256:2304], in_=KF[256:384, :]), r=[KF_b], w=KBab, dma=True)
            A("sp", lambda e: e.dma_start(out=VBa[:, 2:18, :], in_=agT_v[:, :, 256:384]), r=[agTout_b], w=VBab,
              dma=True)
            for ch in range(2):
                heads = []
                for half in range(2):
                    q = (Q[t, "b"][half * 64:half * 64 + 64, ch, :], Qb[t, "b"][ch])
                    tiles = []
                    for kt in range(18):
                        kv = KBa[half * 64:half * 64 + 64, kt * 128:(kt + 1) * 128]
                        tiles.append((kv, KBab[0], VBa[:, kt, half * 64:(half + 1) * 64], VBab[0], 0, 512, None, 0))
                    heads.append(dict(q=q, tiles=tiles, last={0: tiles[-1][0]}))
                attn_chunk(1, "b", ch, heads, l)
            o = l * PV_L
            kdg = pv[:, o + PV_KG + 3:o + PV_KG + 4]
            wk, wv, sbf = mla_load(l)
            A("pool", lambda e: e.dma_start(out=cckvs[:], in_=cckv[l]), w=cckvsb, dma=True)
            A("pool", lambda e: e.dma_start(out=ckpes[:], in_=ckpe[l]), w=ckpesb, dma=True)
            for h in range(4):
                ps, psb = mla_kraw(wk, sbf, h, cckvs[:], cckvsb[0], ckpes[:], ckpesb[0], 256)
                headnorm(ps, psb, 96, CM_ONES96, 1 / 96., kdg, KD[0:96, h, 0:256], KDb[0], N=256)
            for tb in range(2):
                ps, psb = self.gp()
                A("pe", lambda e, tb=tb, ps=ps: e.matmul(ps[:, 0:256], lhsT=cckvs[:, tb * 128:(tb + 1) * 128], rhs=wv,
                                                         start=True, stop=True), r=[sbf, cckvsb[0]], w=[psb])
                A("act", lambda e, tb=tb, ps=ps: e.activation(out=VD[:, tb, :], in_=ps[:, 0:256], func=AF.Identity),
                  r=[psb], w=VDb)
            for h in range(4):
                A("sp", lambda e, h=h: e.dma_start(out=KD[0:96, h, 256:2304], in_=KF[512 + 96 * h:512 + 96 * (h + 1), :]),
                  r=[KF_b], w=KDb, dma=True)
            A("sp", lambda e: e.dma_start(out=VD[:, 2:18, :], in_=agT_v[:, :, 512:768]), r=[agTout_b], w=VDb, dma=True)
            for ch in range(2):
                heads = []
                for half in range(2):
                    h = 2 * ch + half
                    q = (Q[t, "d"][0:96, h, :], Qb[t, "d"][h])
                    tiles = []
                    for kt in range(18):
                        kv = KD[0:96, h, kt * 128:(kt + 1) * 128]
                        tiles.append((kv, KDb[0], VD[:, kt, h * 64:(h + 1) * 64], VDb[0], 0, 512, None, 0))
                    heads.append(dict(q=q, tiles=tiles, last={0: tiles[-1][0]}))
                attn_chunk(1, "d", ch, heads, l)

        def merge(l):
            for c in range(8):
                sa, sab = wload([(lambda s: w3(s, 8, 512),
                                  wgate[l].rearrange("(k p) c n -> p k c n", p=128)[:, :, c, :])])
                va = w3(sa, 8, 512)
                sbr, sbrb = wload([(lambda s: s[:, 0:1024], wbr[l, c])])
                vb = sbr[:, 0:1024].rearrange("p (n k j) -> p n k j", n=4, k=2)
                for t in self.tiles:
                    for n, m in enumerate("abcd"):
                        pg, pgb = proj_fm(va, sab, n * 128, 128, t)
                        gk = self.rot("gate", 2)
                        A("act", lambda e, gk=gk, pg=pg: e.activation(out=gate[gk][:], in_=pg[:], func=AF.Sigmoid),
                          r=[pgb], w=[gateb[gk]])
                        pp, ppb = self.gp()
                        for k2 in range(2):
                            A("pe", lambda e, n=n, k2=k2, pp=pp, m=m, t=t, vb=vb: e.matmul(
                                pp[:], lhsT=vb[:, n, k2, :], rhs=OT[t, m][:, k2, :], start=(k2 == 0), stop=(k2 == 1)),
                              r=[sbrb, OTb[t, m][k2]], w=[ppb])
                        if n == 0:
                            A("dve", lambda e, pp=pp, gk=gk: e.tensor_tensor(out=macc[:], in0=pp[:], in1=gate[gk][:],
                                                                             op=ALU.mult),
                              r=[ppb, gateb[gk]], w=[maccb])
                        else:
                            k = self.rot("tmp", NT)
                            A("dve", lambda e, pp=pp, gk=gk, k=k: e.tensor_tensor(out=tmp[k][:], in0=pp[:],
                                                                                  in1=gate[gk][:], op=ALU.mult),
                              r=[ppb, gateb[gk]], w=[tmpb[k]])
                            if n < 3:
                                A("dve", lambda e, k=k: e.tensor_tensor(out=macc[:], in0=macc[:], in1=tmp[k][:],
                                                                         op=ALU.add),
                                  r=[maccb, tmpb[k]], w=[maccb])
                            else:
                                A("dve", lambda e, k=k, t=t, c=c: e.tensor_tensor(out=merged[t][:, c, :], in0=macc[:],
                                                                                   in1=tmp[k][:], op=ALU.add),
                                  r=[maccb, tmpb[k]], w=[mgb[t][c]])
            for g in range(2):
                slot, sbf = load_kn(w_out[l], g * 512, 512)
                v = w3(slot, 8, 512)
                for j in range(4):
                    c = g * 4 + j
                    for t in self.tiles:
                        ps, psb = self.gp()
                        for kc in range(8):
                            A("pe", lambda e, kc=kc, ps=ps, t=t, j=j, v=v: e.matmul(
                                ps[:], lhsT=v[:, kc, j * 128:(j + 1) * 128], rhs=merged[t][:, kc, :],
                                start=(kc == 0), stop=(kc == 7)), r=[sbf, mgb[t][kc]], w=[psb])
                        A("dve", lambda e, ps=ps, t=t, c=c: e.scalar_tensor_tensor(
                            out=xT[t][:, c, :], in0=ps[:], scalar=mod(l, t, 2, c), in1=xT[t][:, c, :],
                            op0=ALU.mult, op1=ALU.add), r=[psb, xb[t][c], modb[l % 2]], w=[xb[t][c]])

        def halo(l):
            agH_v = agH_in.rearrange("(w p) k -> w p k", p=128)
            for w_, col in ((0, 0), (1, 511)):
                A("sp", lambda e, w_=w_, col=col: e.dma_start(out=agH_v[w_], in_=hT[1][:, :, col], allow_slow_non_contiguous=True),
                  r=hb[1], w=[agHin_b], dma=True)
            A("pool", lambda e: e.collective_compute("AllGather", ALU.bypass, replica_groups=GROUPS,
                                                     ins=[agH_in], outs=[agH_out]),
              r=[agHin_b], w=[agHout_b], dma=True, inc=1, csem="H")
            for j, di in ((0, 4), (1, 5)):
                A("sp", lambda e, j=j, di=di: e.dma_start(out=hh[:, j, :], in_=agH_out[bass.ds(self.dyn[di], 128), :]),
                  r=[agHout_b], w=[hhb], dma=True)
            for j in range(2):
                A("dve", lambda e, j=j: e.tensor_scalar(out=hh[:, j, :], in0=hh[:, j, :], scalar1=pcf[:, j:j + 1],
                                                        scalar2=None, op0=ALU.mult), r=[hhb, pcfb], w=[hhb])

        def ffn(l):
            o = l * PV_L
            cw = lambda j, ch: pv[:, o + PV_CW + j * 44 + ch:o + PV_CW + j * 44 + ch + 1]
            cb = lambda ch: pv[:, o + PV_CB + ch:o + PV_CB + ch + 1]
            for j in range(11):
                slot, sbf = wload([(lambda s: w3(s, 8, 512)[:, :, 0:256],
                                    w_up[l].rearrange("(k p) n -> p k n", p=128)[:, :, j * 256:(j + 1) * 256]),
                                   (lambda s: w3(s, 8, 512)[:, :, 256:512],
                                    w_up[l].rearrange("(k p) n -> p k n", p=128)[:, :, 2816 + j * 256:2816 + (j + 1) * 256])])
                v = w3(slot, 8, 512)
                for t in self.tiles:
                    for q in range(2):
                        ch = 2 * j + q
                        par = 2 * self.rot("ubpar", 2)
                        for part0 in range(2):
                            part = part0 + par
                            cch = ch + 22 * part0
                            c0 = part0 * 256 + q * 128
                            ps, psb = proj_fm(v, sbf, c0, 128, t)
                            A("act", lambda e, ps=ps, part=part: e.activation(out=ub[part][:, 1:513], in_=ps[:],
                                                                             func=AF.Identity),
                              r=[psb], w=[ubb[part]])
                            if t == 1:
                                ph_, phb = self.gp()
                                for kc in range(8):
                                    A("pe", lambda e, kc=kc, ph_=ph_, c0=c0, v=v: e.matmul(
                                        ph_[:, 0:2], lhsT=v[:, kc, c0:c0 + 128], rhs=hh[:, :, kc],
                                        start=(kc == 0), stop=(kc == 7)), r=[sbf, hhb], w=[phb])
                                A("act", lambda e, ph_=ph_, part=part: e.activation(
                                    out=ub[part][:, 0:1], in_=ph_[:, 0:1], func=AF.Identity),
                                  r=[phb, ubb[part]], w=[ubb[part]])
                                A("act", lambda e, ph_=ph_, part=part: e.activation(
                                    out=ub[part][:, 513:514], in_=ph_[:, 1:2], func=AF.Identity),
                                  r=[phb, ubb[part]], w=[ubb[part]])
                            A("act", lambda e, ps=ps, part=part, cch=cch: e.activation(
                                out=cacc[part][:], in_=ps[:], func=AF.Identity, scale=cw(1, cch), bias=cb(cch)),
                              r=[psb, pvb], w=[caccb[part]])
                            if t == 0:
                                segs0 = ((1, 256, 1, 256), (257, 512, 257, 512))
                                segs2 = ((0, 255, 2, 257), (256, 511, 258, 513))
                            else:
                                segs0 = ((0, 512, 0, 512),)
                                segs2 = ((0, 512, 2, 514),)
                            for (a0, a1, u0, u1) in segs0:
                                A("dve", lambda e, part=part, cch=cch, a0=a0, a1=a1, u0=u0, u1=u1: e.scalar_tensor_tensor(
                                    out=cacc[part][:, a0:a1], in0=ub[part][:, u0:u1], scalar=cw(0, cch),
                                    in1=cacc[part][:, a0:a1], op0=ALU.mult, op1=ALU.add),
                                  r=[ubb[part], caccb[part], pvb], w=[caccb[part]])
                            for (a0, a1, u0, u1) in segs2:
                                A("dve", lambda e, part=part, cch=cch, a0=a0, a1=a1, u0=u0, u1=u1: e.scalar_tensor_tensor(
                                    out=cacc[part][:, a0:a1], in0=ub[part][:, u0:u1], scalar=cw(2, cch),
                                    in1=cacc[part][:, a0:a1], op0=ALU.mult, op1=ALU.add),
                                  r=[ubb[part], caccb[part], pvb], w=[caccb[part]])
                        k = self.rot("tmp", NT)
                        A("act", lambda e, k=k, par=par: e.activation(out=tmp[k][:], in_=cacc[par + 1][:], func=AF.Silu),
                          r=[caccb[par + 1]], w=[tmpb[k]])
                        A("dve", lambda e, k=k, t=t, ch=ch, par=par: e.tensor_tensor(out=act[t][:, ch, :], in0=tmp[k][:],
                                                                            in1=cacc[par][:], op=ALU.mult),
                          r=[tmpb[k], caccb[par]], w=[actb[t][ch]])
            for c in range(8):
                slot, sbf = wload([(lambda s: s[:, 0:2816], w_down[l, c])])
                v = w3(slot, 22, 128)
                for t in self.tiles:
                    ps, psb = self.gp()
                    for kc in range(22):
                        A("pe", lambda e, kc=kc, ps=ps, t=t, v=v: e.matmul(
                            ps[:], lhsT=v[:, kc, :], rhs=act[t][:, kc, :], start=(kc == 0), stop=(kc == 21)),
                          r=[sbf, actb[t][kc]], w=[psb])
                    A("dve", lambda e, ps=ps, t=t, c=c: e.scalar_tensor_tensor(
                        out=xT[t][:, c, :], in0=ps[:], scalar=mod(l, t, 5, c), in1=xT[t][:, c, :],
                        op0=ALU.mult, op1=ALU.add), r=[psb, xb[t][c], modb[l % 2]], w=[xb[t][c]])

        ph = getattr(self, "phase", 9)
        for l in range(self.nlayers):
            adaln(l)
            for t in self.tiles:
                norm_h(l, t, 0)
            for grp in range(5):
                slot, sbf = load_kn(wqkv[l], grp * 512, 512)
                v = w3(slot, 8, 512)
                for t in self.tiles[::-1]:
                    qkv(l, t, grp, v, sbf)
            wk, wv, wsb = mla_load(l)
            mla_kv(l, wk, wv, wsb)
            if LAT:
                gather(l)
            self.gn = 4
            ctx_attn(l)
            if LAT:
                ebu_build(l)
                lat_attn(l)
            self.gn = 7
            merge(l)
            for t in self.tiles[::-1]:
                norm_h(l, t, 1)
            if LAT:
                halo(l)
            ffn(l)
        for t in self.tiles:
            self.outs.append(A("sp", lambda e, t=t: e.dma_start(
                out=y[t].rearrange("(c p) n -> p c n", p=128), in_=xT[t][:]), r=xb[t], dma=True))
        A("sp", lambda e: None, extra_deps=self.outs)
        self.S.emit()
        self.st.close()
        return nc


def _prep_shared(I):
    f = np.float32
    w_in = I["w_in"]
    wq = np.zeros((NL, 1024, NQKV), f)
    qa, ka, va = w_in[:, :, 0:256], w_in[:, :, 256:512], w_in[:, :, 512:768]
    qb, kb, vb = w_in[:, :, 768:1024], w_in[:, :, 1024:1152], w_in[:, :, 1152:1280]
    qc, kc, vc = w_in[:, :, 1280:1536], w_in[:, :, 1536:1664], w_in[:, :, 1664:1792]
    qd, ckv, kpe = w_in[:, :, 1792:2176], w_in[:, :, 2176:2304], w_in[:, :, 2304:2336]

    def gq(q):
        h = q.reshape(NL, 1024, 4, 64)
        return h[:, :, [0, 2, 1, 3], :].reshape(NL, 1024, 256)
    wq[:, :, 0:256] = qa
    wq[:, :, 256:512] = ka
    wq[:, :, 512:768] = gq(qb)
    wq[:, :, 768:896] = kb
    wq[:, :, 896:1024] = kc
    wq[:, :, 1024:1280] = gq(qc)
    wq[:, :, 1280:1408] = ckv
    wq[:, :, 1408:1440] = kpe
    for h in range(4):
        wq[:, :, 1536 + h * 128:1536 + h * 128 + 96] = qd[:, :, h * 96:(h + 1) * 96]
    wq[:, :, 2048:2304] = va
    wq[:, :, 2304:2432] = vb
    wq[:, :, 2432:2560] = vc
    wgate = np.ascontiguousarray(
        w_in[:, :, 2336:].reshape(NL, 1024, 4, 8, 128).transpose(0, 1, 3, 2, 4)).reshape(NL, 1024, 8, 512)
    wb = I["w_branch"]
    perm_nat = np.arange(256)
    perm_gq = np.concatenate([np.arange(64) + 64 * h for h in (0, 2, 1, 3)])
    wbr = np.zeros((NL, 8, 128, 4, 2, 128), f)
    for n in range(4):
        pr = perm_gq if n in (1, 2) else perm_nat
        wn = wb[:, n][:, pr, :]
        wn = wn.reshape(NL, 2, 128, 8, 128)
        wbr[:, :, :, n, :, :] = wn.transpose(0, 3, 2, 1, 4)
    wbr = wbr.reshape(NL, 8, 128, 1024)
    wu = I["w_ukv"].reshape(NL, 128, 4, 128)
    wukv = np.zeros((NL, 128, 640), f)
    for h in range(4):
        wukv[:, :, h * 96:h * 96 + 64] = wu[:, :, h, 0:64]
        wukv[:, :, 384 + h * 64:384 + (h + 1) * 64] = wu[:, :, h, 64:128]
    pv = np.zeros((128, PV_N), f)
    t2 = lambda v: np.concatenate([v, v])
    for l in range(NL):
        o = l * PV_L
        pv[:, o + PV_B:o + PV_B + 48] = I["b_ada"][l].reshape(48, 128).T
        pv[:, o + PV_N1:o + PV_N1 + 8] = I["norm1"][l].reshape(8, 128).T
        pv[:, o + PV_N2:o + PV_N2 + 8] = I["norm2"][l].reshape(8, 128).T
        for j in range(3):
            pv[:, o + PV_CW + j * 44:o + PV_CW + (j + 1) * 44] = I["conv_w"][l, j].reshape(44, 128).T
        pv[:, o + PV_CB:o + PV_CB + 44] = I["conv_b"][l].reshape(44, 128).T
        pv[:, o + PV_QG + 0] = t2(I["qn_a"][l])
        pv[:, o + PV_QG + 1] = t2(I["qn_b"][l])
        pv[:, o + PV_QG + 2] = t2(I["qn_c"][l])
        pv[0:96, o + PV_QG + 3] = I["qn_d"][l]
        pv[:, o + PV_KG + 0] = t2(I["kn_a"][l])
        pv[:, o + PV_KG + 1] = t2(I["kn_b"][l])
        pv[:, o + PV_KG + 2] = t2(I["kn_c"][l])
        pv[0:96, o + PV_KG + 3] = I["kn_d"][l]
        pv[:, o + PV_KG + 4] = I["kvn_d"][l]
        sk = I["sink_c"][l]
        for ch in range(2):
            pv[0:64, o + PV_SK + ch] = sk[ch]
            pv[64:128, o + PV_SK + ch] = sk[ch + 2]
    pv[:, PV_EPS] = EPS
    oh2 = np.zeros((64, 64, 128), f)
    for qc in range(64):
        for kp in range(128):
            dc = int(np.clip((kp % 64) - qc, -15, 15)) + 15
            oh2[(kp // 64) * 32 + dc, qc, kp] = 1.0
    return dict(oh2=oh2.reshape(64, 8192), rpb=np.ascontiguousarray(I["rpb_a"]),
                w_ada=np.ascontiguousarray(I["w_ada"]), wqkv=wq, wgate=wgate, wbr=wbr,
                w_out=np.ascontiguousarray(I["w_out"]), w_up=np.ascontiguousarray(I["w_up"]),
                w_down=np.ascontiguousarray(I["w_down"].reshape(NL, 22, 128, 8, 128).transpose(0, 3, 2, 1, 4)).reshape(NL, 8, 128, 2816), wukv=wukv, pv=pv, cm=_host_consts())


def _core_consts(ql):
    f = np.float32
    s = 512 * ql
    tpos = s + np.arange(512)
    row = (tpos // 64).astype(f)
    col = (tpos % 64).astype(f)

    def cs_tab(dim):
        half = dim // 2
        inv = (10000.0 ** (-np.arange(0, half, 2, dtype=f) / half)).astype(f)
        r = row[:, None] * inv[None, :]
        c = col[:, None] * inv[None, :]
        ang = np.concatenate([r, r, c, c], axis=-1)
        return np.cos(ang).T.astype(f), np.sin(ang).T.astype(f)
    c64, s64 = cs_tab(64)
    c32, s32 = cs_tab(32)
    cs = np.zeros((128, 4, 512), f)
    cs[:, 0, :] = np.concatenate([c64, c64], 0)
    cs[:, 1, :] = np.concatenate([s64, s64], 0)
    cs[0:64, 2, :] = 1.0
    cs[64:96, 2, :] = c32
    cs[64:96, 3, :] = s32
    rw0 = int(np.clip(8 * ql - 4, 0, 16))
    ta0 = 64 * rw0
    tc0 = int(np.clip(512 * ql - 128, 0, 1280))
    maskA = np.zeros((128, 8, 512), f)
    kp = np.arange(128)
    hf, kcol = kp // 64, kp % 64
    qi, qc = np.arange(512) // 64, np.arange(512) % 64
    qr = 8 * ql + qi
    r0 = np.clip(qr - 4, 0, 24)
    c0 = np.clip(qc - 8, 0, 48)
    colok = (kcol[:, None] >= c0[None, :]) & (kcol[:, None] < c0[None, :] + 16)
    for kt in range(8):
        kr = rw0 + 2 * kt + hf
        rowok = (kr[:, None] >= r0[None, :]) & (kr[:, None] < r0[None, :] + 8)
        maskA[:, kt, :] = (rowok & colok).astype(f)
    maskC = np.zeros((128, 6, 512), f)
    for j in range(6):
        ktok = tc0 + 128 * j + kp
        maskC[:, j, :] = (np.abs(ktok[:, None] - tpos[None, :]) <= 128).astype(f)
    rsel = np.zeros((15, 2, NU), f)
    for h2 in range(2):
        for u in range(NU):
            dri = rw0 - 8 * ql + 21 + h2 - u
            if 0 <= dri <= 14:
                rsel[dri, h2, u] = 1.0
    pcf = np.zeros((128, 2), f)
    pcf[:, 0] = 1.0 if ql > 0 else 0.0
    pcf[:, 1] = 1.0 if ql < 3 else 0.0
    meta = np.zeros((1, 8), np.int32)
    meta[0, :6] = [ta0 // 128, tc0 // 128, ta0, tc0, 128 * (2 * max(ql - 1, 0) + 1), 128 * (2 * min(ql + 1, 3))]
    return dict(cs=cs, maskA=maskA, maskC=maskC, rsel=rsel.reshape(15, 2 * NU), pcf=pcf, meta=meta)


_NC_CACHE = {}


def _core_inputs(I, shared, r):
    bl, ql = r // 4, r % 4
    xin = np.zeros((2, 1024, 512), np.float32)
    xin[0] = I["x_prompt"][2 * r:2 * r + 2].reshape(512, 1024).T
    xin[1] = I["x_sample"][bl, 512 * ql:512 * ql + 512].T
    cond = np.zeros((128, 16), np.float32)
    cond[:, 0::2] = I["c_ctx"].reshape(8, 128).T
    cond[:, 1::2] = I["c"][bl].reshape(8, 128).T
    d = dict(shared)
    d["xin"] = xin
    d["cond"] = cond
    d.update(_core_consts(ql))
    tr = lambda a, F: np.ascontiguousarray(a.reshape(NL, 256, F).transpose(0, 2, 1))
    fl = lambda a, F: np.ascontiguousarray(a.reshape(NL, 256, F))
    d["cka"] = tr(I["cache_nat_k"][bl], 256)
    d["cva"] = fl(I["cache_nat_v"][bl], 256)
    d["ckb"] = tr(I["cache_gqa_k"][bl], 128)
    d["cvb"] = fl(I["cache_gqa_v"][bl], 128)
    d["ckc"] = tr(I["cache_win_k"][bl], 128)
    d["cvc"] = fl(I["cache_win_v"][bl], 128)
    d["cckv"] = tr(I["cache_mla_ckv"][bl], 128)
    d["ckpe"] = tr(I["cache_mla_kpe"][bl], 32)
    return d


def kernel(**I):
    import os
    I = {k: np.asarray(v) for k, v in I.items()}
    shared = _prep_shared(I)
    key = "p"
    if key not in _NC_CACHE:
        p = Prog((0, 1))
        p.nlayers = int(os.environ.get("K_NL", NL))
        _NC_CACHE[key] = p.build()
    nc = _NC_CACHE[key]
    in_maps = [_core_inputs(I, shared, r) for r in range(8)]
    res = run_bass_kernel_spmd(nc, in_maps, core_ids=list(range(8)))
    R = res.results
    f = np.float32
    yp = np.zeros((16, 256, 1024), f)
    ys = np.zeros((2, 2048, 1024), f)
    nat_k = np.zeros((16, NL, 256, 4, 64), f)
    nat_v = np.zeros((16, NL, 256, 4, 64), f)
    gqa_k = np.zeros((16, NL, 256, 2, 64), f)
    gqa_v = np.zeros((16, NL, 256, 2, 64), f)
    win_k = np.zeros((16, NL, 256, 2, 64), f)
    win_v = np.zeros((16, NL, 256, 2, 64), f)
    mckv = np.zeros((16, NL, 256, 128), f)
    mkpe = np.zeros((16, NL, 256, 32), f)
    for r in range(8):
        o = R[r]
        bl, ql = r // 4, r % 4
        yp[2 * r:2 * r + 2] = o["y"][0].T.reshape(2, 256, 1024)
        ys[bl, 512 * ql:512 * ql + 512] = o["y"][1].T

        def fm(a, nh):
            L, F, _ = a.shape
            return a.reshape(L, F, 2, 256).transpose(2, 0, 3, 1).reshape(2, L, 256, nh, F // nh)

        def tm(a, nh):
            L, _, F = a.shape
            return a.reshape(L, 2, 256, F).transpose(1, 0, 2, 3).reshape(2, L, 256, nh, F // nh)
        nat_k[2 * r:2 * r + 2] = fm(o["ok_a"], 4)
        nat_v[2 * r:2 * r + 2] = tm(o["ov_a"], 4)
        gqa_k[2 * r:2 * r + 2] = fm(o["ok_b"], 2)
        gqa_v[2 * r:2 * r + 2] = tm(o["ov_b"], 2)
        win_k[2 * r:2 * r + 2] = fm(o["ok_c"], 2)
        win_v[2 * r:2 * r + 2] = tm(o["ov_c"], 2)
        mckv[2 * r:2 * r + 2] = fm(o["o_ckv"], 1).reshape(2, NL, 256, 128)
        mkpe[2 * r:2 * r + 2] = fm(o["o_kpe"], 1).reshape(2, NL, 256, 32)
    return (yp, ys, nat_k, nat_v, gqa_k, gqa_v, win_k, win_v, mckv, mkpe)
```
